# Optimizing a Trainium2 kernel written in Bass

```python
import jax
import jax.numpy as jnp
from jax import lax
import numpy as np

D_MODEL = 1024
BATCH = 2
SEQ = 8192
DEPTH = 1

GRID_W = 64
CTX_LEN = 256
RET_HEADS = 4
RET_DK = 128
RET_DV = 128
RET_CHUNK = 128
GLA_HEADS = 4
GLA_DK = 64
GLA_DV = 128
GLA_RANK = 16
GLA_TAU = 16.0
GLA_CHUNK = 64
RET_WIDTH = RET_HEADS * RET_DV
GLA_WIDTH = GLA_HEADS * GLA_DV
D_MIX = RET_WIDTH + GLA_WIDTH
D_FF = 2816
N_MOD = 9
ROPE_BASE = 10000.0
EPS = 1e-6
IN_WIDTHS = (RET_HEADS * RET_DK, RET_HEADS * RET_DK, RET_WIDTH, RET_WIDTH,
             GLA_HEADS * GLA_DK, GLA_HEADS * GLA_DK, GLA_WIDTH, GLA_WIDTH, GLA_RANK, GLA_RANK)
IN_COLS = sum(IN_WIDTHS)
IN_SPLITS = tuple(int(v) for v in np.cumsum(IN_WIDTHS)[:-1])

kernel_name = "hybrid_retention_gla_macaron_dit"


def rms_norm(x, w):
    xf = x.astype(jnp.float32)
    y = xf * lax.rsqrt(jnp.mean(xf * xf, axis=-1, keepdims=True) + EPS)
    return (y * w.astype(jnp.float32)).astype(x.dtype)


def modulate(h, shift, scale):
    return h * (1.0 + scale[:, None, :]) + shift[:, None, :]


def swiglu(h, w1, w3, w2):
    return (jax.nn.silu(h @ w1) * (h @ w3)) @ w2


def split_heads(t, n_heads):
    b, n, _ = t.shape
    return t.reshape(b, n, n_heads, -1).transpose(0, 2, 1, 3).astype(jnp.float32)


def merge_heads(t):
    b, h, n, d = t.shape
    return t.transpose(0, 2, 1, 3).reshape(b, n, h * d)


def rotate(x, ang):
    x1, x2 = jnp.split(x, 2, axis=-1)
    cos, sin = jnp.cos(ang), jnp.sin(ang)
    return jnp.concatenate([x1 * cos - x2 * sin, x1 * sin + x2 * cos], axis=-1)


def grid_rope(x, n_tok):
    rows = n_tok // GRID_W
    row = jnp.repeat(jnp.arange(rows, dtype=jnp.float32), GRID_W)
    col = jnp.tile(jnp.arange(GRID_W, dtype=jnp.float32), rows)
    n_freq = x.shape[-1] // 4
    freqs = ROPE_BASE ** (-jnp.arange(n_freq, dtype=jnp.float32) / n_freq)
    xr, xc = jnp.split(x, 2, axis=-1)
    return jnp.concatenate([rotate(xr, row[:, None] * freqs), rotate(xc, col[:, None] * freqs)], axis=-1)


def to_chunks(t, c):
    b, h, n, d = t.shape
    return t.reshape(b, h, n // c, c, d).transpose(2, 0, 1, 3, 4)


def from_chunks(t):
    nc, b, h, c, d = t.shape
    return t.transpose(1, 2, 0, 3, 4).reshape(b, h, nc * c, d)


def retention_chunked(q, k, v, s0, log_gamma):
    c = RET_CHUNK
    pos = jnp.arange(c, dtype=jnp.float32)
    diff = pos[:, None] - pos[None, :]
    lg = log_gamma[:, None, None]
    decay = jnp.exp(jnp.where(diff >= 0, diff * lg, -jnp.inf))
    q_dec = jnp.exp((pos + 1.0) * log_gamma[:, None])[..., None]
    k_dec = jnp.exp((c - 1.0 - pos) * log_gamma[:, None])[..., None]
    chunk_dec = jnp.exp(c * log_gamma)[:, None, None]

    def step(s, xs):
        qc, kc, vc = xs
        scores = jnp.einsum('bhid,bhjd->bhij', qc, kc) * decay
        o = jnp.einsum('bhij,bhje->bhie', scores, vc) + jnp.einsum('bhid,bhde->bhie', qc * q_dec, s)
        s = chunk_dec * s + jnp.einsum('bhjd,bhje->bhde', kc * k_dec, vc)
        return s, o

    s_fin, o = lax.scan(step, s0, (to_chunks(q, c), to_chunks(k, c), to_chunks(v, c)))
    return from_chunks(o), s_fin


def gla_chunked(q, k, v, g, s0):
    c = GLA_CHUNK
    tri = jnp.tril(jnp.ones((c, c), dtype=bool))

    def step(s, xs):
        qc, kc, vc, gc = xs
        b = jnp.cumsum(gc, axis=-2)
        rel = b[:, :, :, None, :] - b[:, :, None, :, :]
        rel = jnp.exp(jnp.where(tri[:, :, None], rel, -jnp.inf))
        scores = jnp.einsum('bhid,bhjd,bhijd->bhij', qc, kc, rel)
        o = jnp.einsum('bhij,bhje->bhie', scores, vc) + jnp.einsum('bhid,bhde->bhie', qc * jnp.exp(b), s)
        b_last = b[:, :, -1:, :]
        s = jnp.exp(b_last[:, :, 0, :])[..., None] * s + jnp.einsum('bhjd,bhje->bhde', kc * jnp.exp(b_last - b), vc)
        return s, o

    s_fin, o = lax.scan(step, s0, (to_chunks(q, c), to_chunks(k, c), to_chunks(v, c), to_chunks(g, c)))
    return from_chunks(o), s_fin


def bidir_prefix(scan_f, scan_b, lat_f, lat_b, ctx_f, ctx_b):
    flip = lambda ts: tuple(t[:, :, ::-1] for t in ts)
    q0, v0 = ctx_f[0], ctx_f[2]
    s0 = jnp.zeros(q0.shape[:2] + (q0.shape[-1], v0.shape[-1]), jnp.float32)
    o_cf, s_f = scan_f(*ctx_f, s0)
    o_cb, s_b = scan_b(*flip(ctx_b), s0)
    o_lf, _ = scan_f(*lat_f, s_f)
    o_lb, _ = scan_b(*flip(lat_b), s_b)
    return o_lf + o_lb[:, :, ::-1], o_cf + o_cb[:, :, ::-1]


def retention_inputs(p, n_tok):
    q = split_heads(p[0], RET_HEADS) * (RET_DK ** -0.5)
    k = split_heads(p[1], RET_HEADS)
    if n_tok is not None:
        q, k = grid_rope(q, n_tok), grid_rope(k, n_tok)
    return (q, k, split_heads(p[2], RET_HEADS))


def gla_inputs(p, w_f, b_f, w_b, b_b):
    q = split_heads(p[4], GLA_HEADS) * (GLA_DK ** -0.5)
    k = split_heads(p[5], GLA_HEADS)
    v = split_heads(p[6], GLA_HEADS)
    g_f = split_heads(jax.nn.log_sigmoid((p[8] @ w_f + b_f).astype(jnp.float32)), GLA_HEADS) / GLA_TAU
    g_b = split_heads(jax.nn.log_sigmoid((p[9] @ w_b + b_b).astype(jnp.float32)), GLA_HEADS) / GLA_TAU
    return (q, k, v, g_f), (q, k, v, g_b)


def merge_outputs(o_ret, o_gla, p, ret_norm_w, gla_norm_w, w_out, dtype):
    mu = jnp.mean(o_ret, axis=-1, keepdims=True)
    var = jnp.mean(jnp.square(o_ret - mu), axis=-1, keepdims=True)
    r = merge_heads((o_ret - mu) * lax.rsqrt(var + EPS)) * ret_norm_w * jax.nn.silu(p[3].astype(jnp.float32))
    gl = merge_heads(o_gla * lax.rsqrt(jnp.mean(o_gla * o_gla, axis=-1, keepdims=True) + EPS))
    gl = gl * gla_norm_w * jax.nn.silu(p[7].astype(jnp.float32))
    return jnp.concatenate([r, gl], axis=-1).astype(dtype) @ w_out


def token_mixing(h_lat, h_ctx, w_in, ret_decay_f, ret_decay_b, ret_norm_w,
                 gla_gate_w_f, gla_gate_b_f, gla_gate_w_b, gla_gate_b_b, gla_norm_w, w_out, with_ctx):
    n_lat = h_lat.shape[1]
    p_lat = jnp.split(h_lat @ w_in, IN_SPLITS, axis=-1)
    p_ctx = jnp.split(h_ctx @ w_in, IN_SPLITS, axis=-1)
    lg_f = jax.nn.log_sigmoid(ret_decay_f.astype(jnp.float32))
    lg_b = jax.nn.log_sigmoid(ret_decay_b.astype(jnp.float32))
    ret_lat = retention_inputs(p_lat, n_lat)
    ret_ctx = retention_inputs(p_ctx, None)
    o_ret_l, o_ret_c = bidir_prefix(lambda q, k, v, s: retention_chunked(q, k, v, s, lg_f),
                                    lambda q, k, v, s: retention_chunked(q, k, v, s, lg_b),
                                    ret_lat, ret_lat, ret_ctx, ret_ctx)
    gla_lat_f, gla_lat_b = gla_inputs(p_lat, gla_gate_w_f, gla_gate_b_f, gla_gate_w_b, gla_gate_b_b)
    gla_ctx_f, gla_ctx_b = gla_inputs(p_ctx, gla_gate_w_f, gla_gate_b_f, gla_gate_w_b, gla_gate_b_b)
    o_gla_l, o_gla_c = bidir_prefix(gla_chunked, gla_chunked, gla_lat_f, gla_lat_b, gla_ctx_f, gla_ctx_b)
    y_lat = merge_outputs(o_ret_l, o_gla_l, p_lat, ret_norm_w, gla_norm_w, w_out, h_lat.dtype)
    y_ctx = None
    if with_ctx:
        y_ctx = merge_outputs(o_ret_c, o_gla_c, p_ctx, ret_norm_w, gla_norm_w, w_out, h_ctx.dtype)
    return y_lat, y_ctx


def setup_inputs(seed: int = 0) -> dict:
    key = jax.random.key(seed)
    ks = jax.random.split(key, 32)
    f32 = jnp.float32

    def nrm(k, shape, scale):
        return jax.random.normal(k, shape, f32) * scale

    def gain(k, shape):
        return 1.0 + 0.05 * jax.random.normal(k, shape, f32)

    decay_logit = jnp.log(2.0 ** (5.0 + jnp.arange(RET_HEADS, dtype=f32)) - 1.0)
    return {
        "x": nrm(ks[0], (BATCH, SEQ, D_MODEL), 1.0),
        "c": nrm(ks[1], (BATCH, D_MODEL), 1.0),
        "ctx": nrm(ks[2], (BATCH, CTX_LEN, D_MODEL), 1.0),
        "c_ctx": nrm(ks[3], (D_MODEL,), 1.0),
        "ada_w": nrm(ks[4], (DEPTH, D_MODEL, N_MOD * D_MODEL), 0.5 * D_MODEL ** -0.5),
        "ada_b": nrm(ks[5], (DEPTH, N_MOD * D_MODEL), 0.02),
        "norm1_w": gain(ks[6], (DEPTH, D_MODEL)),
        "ffn1_w1": nrm(ks[7], (DEPTH, D_MODEL, D_FF), D_MODEL ** -0.5),
        "ffn1_w3": nrm(ks[8], (DEPTH, D_MODEL, D_FF), D_MODEL ** -0.5),
        "ffn1_w2": nrm(ks[9], (DEPTH, D_FF, D_MODEL), D_FF ** -0.5),
        "norm2_w": gain(ks[10], (DEPTH, D_MODEL)),
        "w_in": nrm(ks[11], (DEPTH, D_MODEL, IN_COLS), D_MODEL ** -0.5),
        "ret_decay_f": decay_logit + 0.1 * jax.random.normal(ks[12], (DEPTH, RET_HEADS), f32),
        "ret_decay_b": decay_logit + 0.1 * jax.random.normal(ks[13], (DEPTH, RET_HEADS), f32),
        "ret_norm_w": gain(ks[14], (DEPTH, RET_WIDTH)),
        "gla_gate_w_f": nrm(ks[15], (DEPTH, GLA_RANK, GLA_HEADS * GLA_DK), GLA_RANK ** -0.5),
        "gla_gate_b_f": 1.0 + 0.5 * jax.random.normal(ks[16], (DEPTH, GLA_HEADS * GLA_DK), f32),
        "gla_gate_w_b": nrm(ks[17], (DEPTH, GLA_RANK, GLA_HEADS * GLA_DK), GLA_RANK ** -0.5),
        "gla_gate_b_b": 1.0 + 0.5 * jax.random.normal(ks[18], (DEPTH, GLA_HEADS * GLA_DK), f32),
        "gla_norm_w": gain(ks[19], (DEPTH, GLA_WIDTH)),
        "w_out": nrm(ks[20], (DEPTH, D_MIX, D_MODEL), D_MIX ** -0.5),
        "norm3_w": gain(ks[21], (DEPTH, D_MODEL)),
        "ffn2_w1": nrm(ks[22], (DEPTH, D_MODEL, D_FF), D_MODEL ** -0.5),
        "ffn2_w3": nrm(ks[23], (DEPTH, D_MODEL, D_FF), D_MODEL ** -0.5),
        "ffn2_w2": nrm(ks[24], (DEPTH, D_FF, D_MODEL), D_FF ** -0.5),
        "final_norm_w": gain(ks[25], (D_MODEL,)),
    }


def reference(x, c, ctx, c_ctx, ada_w, ada_b, norm1_w, ffn1_w1, ffn1_w3, ffn1_w2, norm2_w, w_in,
              ret_decay_f, ret_decay_b, ret_norm_w, gla_gate_w_f, gla_gate_b_f, gla_gate_w_b, gla_gate_b_b,
              gla_norm_w, w_out, norm3_w, ffn2_w1, ffn2_w3, ffn2_w2, final_norm_w):
    cond_lat = jax.nn.silu(c)
    cond_ctx = jax.nn.silu(c_ctx)[None, :]
    for i in range(DEPTH):
        update_ctx = i < DEPTH - 1
        m_l = jnp.split(cond_lat @ ada_w[i] + ada_b[i], N_MOD, axis=-1)
        m_c = jnp.split(cond_ctx @ ada_w[i] + ada_b[i], N_MOD, axis=-1)
        x = x + 0.5 * m_l[2][:, None, :] * swiglu(modulate(rms_norm(x, norm1_w[i]), m_l[0], m_l[1]),
                                                    ffn1_w1[i], ffn1_w3[i], ffn1_w2[i])
        ctx = ctx + 0.5 * m_c[2][:, None, :] * swiglu(modulate(rms_norm(ctx, norm1_w[i]), m_c[0], m_c[1]),
                                                        ffn1_w1[i], ffn1_w3[i], ffn1_w2[i])
        y_l, y_c = token_mixing(modulate(rms_norm(x, norm2_w[i]), m_l[3], m_l[4]),
                                modulate(rms_norm(ctx, norm2_w[i]), m_c[3], m_c[4]),
                                w_in[i], ret_decay_f[i], ret_decay_b[i], ret_norm_w[i],
                                gla_gate_w_f[i], gla_gate_b_f[i], gla_gate_w_b[i], gla_gate_b_b[i],
                                gla_norm_w[i], w_out[i], update_ctx)
        x = x + m_l[5][:, None, :] * y_l
        x = x + 0.5 * m_l[8][:, None, :] * swiglu(modulate(rms_norm(x, norm3_w[i]), m_l[6], m_l[7]),
                                                    ffn2_w1[i], ffn2_w3[i], ffn2_w2[i])
        if update_ctx:
            ctx = ctx + m_c[5][:, None, :] * y_c
            ctx = ctx + 0.5 * m_c[8][:, None, :] * swiglu(modulate(rms_norm(ctx, norm3_w[i]), m_c[6], m_c[7]),
                                                            ffn2_w1[i], ffn2_w3[i], ffn2_w2[i])
    return rms_norm(x, final_norm_w)
```

```python
from contextlib import ExitStack
import numpy as np
import concourse.bass as bass
import concourse.mybir as mybir
from concourse.bass_utils import run_bass_kernel_spmd

F32 = mybir.dt.float32
BF16 = mybir.dt.bfloat16
ALU = mybir.AluOpType
AF = mybir.ActivationFunctionType

NCORES = 8
D = 1024
KD = 8
NT = 2048
NCTX = 256
DFF = 2816
NFF = 22
INC = 3616
EPS = 1e-6
ENGS = ("pe", "act", "dve", "pool", "sp")
NDMASEM = 8


class Buf:
    __slots__ = ("name", "last_w", "readers", "excl")

    def __init__(self, name="", excl=False):
        self.name = name
        self.last_w = None
        self.readers = {}
        self.excl = excl


class Op:
    __slots__ = ("eng", "fn", "deps", "marked", "sem", "val", "kind", "seq")


class Sched:
    def __init__(self, nc):
        self.nc = nc
        self.ops = {e: [] for e in ENGS}
        self.dma_count = {e: 0 for e in ENGS}
        self.dma_last = {}
        self.cc_count = 0
        self.seq = 0

    @staticmethod
    def _key(o):
        return o.eng if o.kind == "c" else o.sem

    def op(self, eng, fn, reads=(), writes=(), kind="c"):
        o = Op()
        o.eng, o.fn, o.kind = eng, fn, kind
        o.deps, o.marked, o.sem, o.val = [], False, None, 0
        o.seq = self.seq
        self.seq += 1
        deps = []
        for b in reads:
            if b.last_w is not None:
                deps.append((b.last_w, True))
            if b.excl:
                for r in b.readers.values():
                    if r.eng != eng:
                        deps.append((r, False))
        for b in writes:
            if b.last_w is not None:
                deps.append((b.last_w, False))
            for r in b.readers.values():
                deps.append((r, False))
        if kind == "dma":
            k = self.dma_count[eng]
            self.dma_count[eng] += 1
            slot = (eng, k % NDMASEM)
            o.sem = "dma_%s_%d" % slot
            o.val = 16 * (k // NDMASEM + 1)
            o.marked = True
            prev = self.dma_last.get(slot)
            if prev is not None:
                deps.append((prev, True))
            self.dma_last[slot] = o
        elif kind == "cc":
            self.cc_count += 1
            o.sem = "ccsem"
            o.val = self.cc_count
            o.marked = True
        for d, raw in deps:
            if d is o:
                continue
            if d.kind == "c" and d.eng == eng and eng == "pe":
                continue
            d.marked = True
            o.deps.append(d)
        self.ops[eng].append(o)
        key = self._key(o)
        for b in reads:
            b.readers[key] = o
        for b in writes:
            b.last_w = o
            b.readers = {}
        return o

    def fence(self, olds, news):
        col = {}
        for b in olds:
            for o in ([b.last_w] if b.last_w is not None else []) + list(b.readers.values()):
                k = self._key(o)
                if k not in col or col[k].seq < o.seq:
                    col[k] = o
        for n in news:
            for k, o in col.items():
                if k not in n.readers or n.readers[k].seq < o.seq:
                    n.readers[k] = o

    def emit(self, final_waits=()):
        nc = self.nc
        for e in ENGS:
            c = 0
            for o in self.ops[e]:
                if o.kind == "c" and o.marked:
                    c += 1
                    o.sem = "eng_" + e
                    o.val = c
        semnames = set()
        for e in ENGS:
            for o in self.ops[e]:
                if o.marked:
                    semnames.add(o.sem)
        with ExitStack() as es:
            sems = {n: es.enter_context(nc.semaphore(n)) for n in sorted(semnames)}
            block = es.enter_context(nc.Block())
            handles = {"pe": block.tensor, "act": block.scalar, "dve": block.vector,
                       "pool": block.gpsimd, "sp": block.sync}
            for e in ENGS:
                ops = self.ops[e]
                fw = list(final_waits) if e == "sp" else []
                if not ops and not fw:
                    continue

                def body(eng, ops=ops, fw=fw):
                    seen = {}
                    for o in ops:
                        need = {}
                        for d in o.deps:
                            if need.get(d.sem, 0) < d.val:
                                need[d.sem] = d.val
                        for s, v in need.items():
                            if seen.get(s, 0) >= v:
                                continue
                            seen[s] = v
                            eng.wait_ge(sems[s], v)
                        ins = o.fn(eng)
                        if o.marked:
                            ins.then_inc(sems[o.sem], 16 if o.kind == "dma" else 1)
                    for d in fw:
                        if seen.get(d.sem, 0) < d.val:
                            seen[d.sem] = d.val
                            eng.wait_ge(sems[d.sem], d.val)

                handles[e](body)


class Arena:
    def __init__(self, S, tensor_bf16, nbytes):
        self.S = S
        self.t = tensor_bf16
        self.nbytes = nbytes
        self.live = []
        self.hist = []

    def alloc(self, shape, dtype, nbufs=1, name=""):
        esz = 4 if dtype == F32 else 2
        n = int(np.prod(shape))
        size = (n * esz + 31) // 32 * 32
        off = 0
        for (o, s, _) in sorted(self.live):
            if off + size <= o:
                break
            off = max(off, o + s)
        assert off + size <= self.nbytes, "arena overflow: %s need %d at %d" % (name, size, off)
        bufs = [Buf(name + str(i)) for i in range(nbufs)]
        olds = []
        for (o, s, bs) in self.hist:
            if o < off + size and off < o + s:
                olds.extend(bs)
        if olds:
            self.S.fence(olds, bufs)
        rec = (off, size, bufs)
        self.live.append(rec)
        self.hist.append(rec)
        ap = self.t[:, off // 2: off // 2 + n * esz // 2]
        if dtype == F32:
            ap = ap.bitcast(F32)
        if len(shape) == 2:
            ap = ap.rearrange("p (a b) -> p a b", a=shape[0])
        elif len(shape) == 3:
            ap = ap.rearrange("p (a b c) -> p a b c", a=shape[0], b=shape[1])
        return ap, bufs, rec

    def free(self, rec):
        self.live.remove(rec)


NMC = 1412
QS = 128.0 ** -0.5
LNQS = float(np.log(QS))
RG = [[0, 1, 2, 3], [4, 5, 6, 7]]


def build_program(stage="full", FG=2):
    nc = bass.Bass("TRN2", target_bir_lowering=False)

    def din(name, shape, dt=F32):
        return nc.dram_tensor(name, list(shape), dt, kind="ExternalInput").ap()

    xT = din("xT", [D, NT])
    ctxT = din("ctxT", [D, NCTX])
    cond = din("cond", [128, 16])
    vecs = din("vecs", [128, 104])
    ada_w = din("ada_w", [D, 9 * D])
    f1w1 = din("f1w1", [D, DFF]); f1w3 = din("f1w3", [D, DFF]); f1w2 = din("f1w2", [DFF, D])
    f2w1 = din("f2w1", [D, DFF]); f2w3 = din("f2w3", [D, DFF]); f2w2 = din("f2w2", [DFF, D])
    consts_bf = din("consts_bf", [128, 256])
    w_in = din("w_in", [D, INC])
    w_out = din("w_out", [D, D])
    rowv = din("rowv", [1, 1544])
    wg = din("wg", [32, 512])
    mixc = din("mixc", [128, NMC])
    rope = din("rope", [16, 128, 1024])
    rmask = din("rmask", [1, 8])
    outT = nc.dram_tensor("outT", [D, NT], F32, kind="ExternalOutput").ap()
    ccR_in = nc.dram_tensor("ccR_in", [128, 1024], F32).ap()
    ccR_out = nc.dram_tensor("ccR_out", [512, 1024], F32).ap()
    ccG_in = nc.dram_tensor("ccG_in", [128, 516], F32).ap()
    ccG_out = nc.dram_tensor("ccG_out", [512, 516], F32).ap()

    es = ExitStack()
    with es:
        def sb(name, shape, dt):
            return es.enter_context(nc.sbuf_tensor(name, shape, dt))

        X = sb("X", [128, KD, NT], F32)
        CB = sb("CB", [128, 256], BF16)
        VEC = sb("VEC", [128, 104], F32)
        CND = sb("CND", [128, 16], F32)
        SCOND = sb("SCOND", [128, KD, 2], BF16)
        MODS = sb("MODS", [128, 72, 2], F32)
        MA = sb("MA", [128, 3, KD, 2], F32)
        MG = sb("MG", [128, 3, KD, 2], F32)
        ARENA_BYTES = 144896
        SCR = sb("SCR", [128, ARENA_BYTES // 2], BF16)
        PS = [es.enter_context(nc.psum_tensor("ps%d" % i, [128, 512], F32)) for i in range(8)]

        S = Sched(nc)
        AR = Arena(S, SCR, ARENA_BYTES)
        ident = CB[:, 0:128]
        ones = CB[:, 128:256]

        bX = [[Buf() for t in range(16)] for k in range(KD)]
        bCB, bVEC, bCND, bSC, bMODS, bMA, bMG = (Buf(n) for n in "CB VEC CND SC MODS MA MG".split())
        bPS = [Buf("ps%d" % i, excl=True) for i in range(8)]

        def xb(k, t0, t1):
            return bX[k][t0 // 128:(t1 + 127) // 128]

        def MM(out, lhsT, rhs, start, stop, r, w):
            return S.op("pe", lambda e: e.matmul(out, lhsT, rhs, start=start, stop=stop), r, w)

        def TR(out, in_, r, w):
            return S.op("pe", lambda e: e.transpose(out, in_, ident), list(r) + [bCB], w)

        def TT(eng, out, a, b, op, r, w):
            return S.op(eng, lambda e: e.tensor_tensor(out, a, b, op), r, w)

        def TS(eng, out, a, s1, s2, op0, op1, r, w):
            if s2 is None:
                return S.op(eng, lambda e: e.tensor_scalar(out, a, s1, None, op0), r, w)
            return S.op(eng, lambda e: e.tensor_scalar(out, a, s1, s2, op0, op1), r, w)

        def STT(eng, out, a, s, b, op0, op1, r, w):
            return S.op(eng, lambda e: e.scalar_tensor_tensor(out, a, s, b, op0, op1), r, w)

        def ACTV(out, in_, func, r, w, bias=None, scale=None):
            kw = {}
            if bias is not None:
                kw["bias"] = bias
            if scale is not None:
                kw["scale"] = scale
            return S.op("act", lambda e: e.activation(out, in_, func, **kw), r, w)

        def CP(eng, out, in_, r, w):
            if eng == "act":
                return S.op("act", lambda e: e.activation(out, in_, AF.Copy), r, w)
            return S.op(eng, lambda e: e.tensor_copy(out, in_), r, w)

        def RED(eng, out, in_, r, w):
            return S.op(eng, lambda e: e.tensor_reduce(out, in_, mybir.AxisListType.X, ALU.add), r, w)

        def DMA(eng, out, in_, r, w):
            return S.op(eng, lambda e: e.dma_start(out=out, in_=in_), r, w, kind="dma")

        def bc(ap, axis, shape):
            return ap.unsqueeze(axis).broadcast_to(list(shape))

        DMA("sp", CND[:], cond, [], [bCND])
        DMA("sp", VEC[:], vecs, [], [bVEC])
        DMA("pool", CB[:], consts_bf, [], [bCB])
        XC, bXCl, xc_rec = AR.alloc([KD, NCTX], F32, nbufs=KD, name="XC")
        bXC = [[bXCl[k]] for k in range(KD)]

        ACTV(SCOND[:].rearrange("p k j -> p j k"), CND[:].rearrange("p (j k) -> p j k", j=2), AF.Silu, [bCND], [bSC])
        aw = [AR.alloc([KD, 1024], BF16, name="adaw%d" % i) for i in range(2)]
        mods_ps = PS[7][:, 0:144].rearrange("p (a b) -> p a b", b=2)
        MODS4 = MODS[:].rearrange("p (m k) j -> p m k j", k=KD)

        def ada_mod(m):
            a_ap, a_b, _ = aw[m % 2]
            DMA("pool", a_ap, ada_w[:, m * 1024:(m + 1) * 1024].rearrange("(k p) c -> p k c", p=128), [], [a_b[0]])
            for j in range(8):
                for k in range(KD):
                    MM(mods_ps[:, m * 8 + j, :], a_ap[:, k, j * 128:(j + 1) * 128], SCOND[:, k, :], k == 0, k == KD - 1,
                       [a_b[0], bSC], [bPS[7]])

        def ada_finish(m0, m1, norms):
            TT("dve", MODS[:, m0 * 8:m1 * 8, :], mods_ps[:, m0 * 8:m1 * 8, :],
               bc(VEC[:, 32 + m0 * 8:32 + m1 * 8], 2, [128, (m1 - m0) * 8, 2]), ALU.add, [bPS[7], bVEC], [bMODS])
            for n in norms:
                STT("dve", MA[:, n], MODS4[:, 3 * n + 1], 1.0, bc(VEC[:, 8 * n:8 * n + 8], 2, [128, KD, 2]), ALU.add, ALU.mult,
                    [bMODS, bVEC], [bMA])
                TS("dve", MG[:, n], MODS4[:, 3 * n + 2], 1.0 if n == 1 else 0.5, None, ALU.mult, None, [bMODS], [bMG])

        for m in range(3):
            ada_mod(m)
        ada_finish(0, 3, [0])

        for T in range(4):
            DMA("sp", X[:, :, T * 512:(T + 1) * 512], xT[:, T * 512:(T + 1) * 512].rearrange("(k p) t -> p k t", p=128),
                [], [b for k in range(KD) for b in xb(k, T * 512, T * 512 + 512)])
        DMA("sp", XC, ctxT.rearrange("(k p) t -> p k t", p=128), [], [b for k in range(KD) for b in bXC[k]])

        def normalize(n, h, hb, hc, hcb, with_ctx):
            sq, sqb, sq_rec = AR.alloc([KD, 512], BF16, name="sq")
            rs, rsb, rs_rec = AR.alloc([512], F32, name="rstd")
            tm = [AR.alloc([512], F32, name="tmn%d" % i) for i in range(4)]
            tiles = [("l", T, 512) for T in range(4)] + ([("c", 0, NCTX)] if with_ctx else [])
            ss = PS[7]
            cnt = 0
            for (kind, T, W) in tiles:
                src = X[:, :, T * 512:(T + 1) * 512] if kind == "l" else XC
                srcb = (lambda k, T=T: xb(k, T * 512, T * 512 + 512)) if kind == "l" else (lambda k: bXC[k])
                j = 0 if kind == "l" else 1
                ACTV(sq[:, :, 0:W], src, AF.Square, [b for k in range(KD) for b in srcb(k)], [sqb[0]])
                for k in range(KD):
                    MM(ss[:, 0:W], ones, sq[:, k, 0:W], k == 0, k == KD - 1, [sqb[0], bCB], [bPS[7]])
                ACTV(rs[:, 0:W], ss[:, 0:W], AF.Ln, [bPS[7]], [rsb[0]], bias=EPS, scale=1.0 / D)
                ACTV(rs[:, 0:W], rs[:, 0:W], AF.Exp, [rsb[0]], [rsb[0]], scale=-0.5)
                for k in range(KD):
                    t_ap, t_b, _ = tm[cnt % 4]
                    cnt += 1
                    STT("dve", t_ap[:, 0:W], src[:, k, :], MA[:, n, k, j:j + 1], rs[:, 0:W], ALU.mult, ALU.mult,
                        srcb(k) + [bMA, rsb[0]], [t_b[0]])
                    dst = h[:, k, T * 512:(T + 1) * 512] if kind == "l" else hc[:, k, :]
                    dstb = hb[k * 4 + T] if kind == "l" else hcb[k]
                    ACTV(dst, t_ap[:, 0:W], AF.Identity, [t_b[0], bMODS], [dstb], bias=MODS4[:, 3 * n, k, j:j + 1])
            AR.free(sq_rec); AR.free(rs_rec)
            for t in tm:
                AR.free(t[2])

        def ffn(n, w1, w3, w2, with_ctx, hook=None):
            h, hb, h_rec = AR.alloc([KD, NT], BF16, nbufs=KD * 4, name="h")
            hc = hcb = hc_rec = None
            if with_ctx:
                hc, hcb, hc_rec = AR.alloc([KD, NCTX], BF16, nbufs=KD, name="hc")
            normalize(n, h, hb, hc, hcb, with_ctx)
            tiles = [("l", T, 512) for T in range(4)] + ([("c", 0, NCTX)] if with_ctx else [])
            NG = NFF // FG
            NW = 3
            wb = [(AR.alloc([KD, FG * 128], BF16, name="w1g%d" % i), AR.alloc([KD, FG * 128], BF16, name="w3g%d" % i),
                   AR.alloc([FG, D], BF16, name="w2g%d" % i)) for i in range(NW)]
            sbuf = [AR.alloc([512], F32, name="silu%d" % i) for i in range(2)]
            gbuf = [AR.alloc([FG, 512], BF16, nbufs=FG, name="g%d" % i) for i in range(2)]
            ucnt = ycnt = tcnt = 0
            for G in range(NG):
                (w1a, w1b, _), (w3a, w3b, _), (w2a, w2b, _) = wb[G % NW]
                c0 = G * FG * 128
                DMA("pool", w1a, w1[:, c0:c0 + FG * 128].rearrange("(k p) c -> p k c", p=128), [], [w1b[0]])
                DMA("pool", w3a, w3[:, c0:c0 + FG * 128].rearrange("(k p) c -> p k c", p=128), [], [w3b[0]])
                DMA("pool", w2a, w2[c0:c0 + FG * 128, :].rearrange("(f p) d -> p f d", p=128), [], [w2b[0]])
                if hook is not None:
                    hook(G)
                for (kind, T, W) in tiles:
                    j = 0 if kind == "l" else 1
                    ga, gb, _ = gbuf[tcnt % 2]
                    tcnt += 1
                    for f in range(FG):
                        pu1 = (ucnt % 2) * 2
                        pu3 = pu1 + 1
                        ucnt += 1
                        for (pw, wa, wbb) in ((pu1, w1a, w1b), (pu3, w3a, w3b)):
                            for k in range(KD):
                                hs = h[:, k, T * 512:(T + 1) * 512] if kind == "l" else hc[:, k, :]
                                hbk = hb[k * 4 + T] if kind == "l" else hcb[k]
                                MM(PS[pw][:, 0:W], wa[:, k, f * 128:(f + 1) * 128], hs, k == 0, k == KD - 1,
                                   [wbb[0], hbk], [bPS[pw]])
                        sa, sbb, _ = sbuf[ucnt % 2]
                        ACTV(sa[:, 0:W], PS[pu1][:, 0:W], AF.Silu, [bPS[pu1]], [sbb[0]])
                        TT("dve", ga[:, f, 0:W], PS[pu3][:, 0:W], sa[:, 0:W], ALU.mult, [bPS[pu3], sbb[0]], [gb[f]])
                    for m in range(KD):
                        py = 4 + ycnt % 3
                        ycnt += 1
                        for f in range(FG):
                            MM(PS[py][:, 0:W], w2a[:, f, m * 128:(m + 1) * 128], ga[:, f, 0:W], f == 0, f == FG - 1,
                               [w2b[0], gb[f]], [bPS[py]])
                        dst = X[:, m, T * 512:(T + 1) * 512] if kind == "l" else XC[:, m, :]
                        dstb = xb(m, T * 512, T * 512 + 512) if kind == "l" else bXC[m]
                        STT("dve", dst, PS[py][:, 0:W], MG[:, n, m, j:j + 1], dst, ALU.mult, ALU.add,
                            [bPS[py], bMG] + dstb, dstb)
            for w in wb:
                for a in w:
                    AR.free(a[2])
            for a in sbuf + gbuf:
                AR.free(a[2])
            AR.free(h_rec)
            if with_ctx:
                AR.free(hc_rec)

        def mixing():
            h2, h2b, h2_rec = AR.alloc([KD, NT], BF16, nbufs=KD * 4, name="h2")
            h2c, h2cb, h2c_rec = AR.alloc([KD, NCTX], BF16, nbufs=KD, name="h2c")
            normalize(1, h2, h2b, h2c, h2cb, True)
            AR.free(xc_rec)
            TILES18 = [("c", 0), ("c", 1)] + [("l", t) for t in range(16)]

            def hsrc(kind, t, k):
                if kind == "c":
                    return h2c[:, k, t * 128:(t + 1) * 128], h2cb[k]
                return h2[:, k, t * 128:(t + 1) * 128], h2b[k * 4 + t // 4]

            MIXC, bmx, mixc_rec = AR.alloc([NMC], F32, name="MIXC"); bMIXC = bmx[0]
            DMA("sp", MIXC, mixc, [], [bMIXC])
            ROWV, brv, rowv_rec = AR.alloc([1544], F32, name="ROWV"); bROWV = brv[0]
            DMA("sp", ROWV, rowv.broadcast_to([128, 1544]), [], [bROWV])
            RMK, brm, rmk_rec = AR.alloc([8], F32, name="RMK"); bRMK = brm[0]
            DMA("sp", RMK, rmask.broadcast_to([128, 8]), [], [bRMK])
            SM, bsm, sm_rec = AR.alloc([64], F32, name="SM"); bSM = bsm[0]
            LG = SM[:, 0:8]; KCOL = SM[:, 8:16]; DEC128 = SM[:, 16:24]; D2048M1 = SM[:, 24:32]; TMP8 = SM[:, 32:40]
            c_127mp = MIXC[:, 0:1]; c_p = MIXC[:, 1:2]; negcol = MIXC[:, 2:3]
            ROWIP1 = MIXC[:, 4:132]; ROW128MI = MIXC[:, 132:260]
            DPOS = MIXC[:, 260:388]; DNEG = MIXC[:, 388:516]; UST = MIXC[:, 516:644]; LST = MIXC[:, 644:772]; I2 = MIXC[:, 772:900]
            MU = MIXC[:, 900:1028]; ML = MIXC[:, 1028:1156]; TRIU = MIXC[:, 1156:1284]; TRIL = MIXC[:, 1284:1412]
            ACTV(TMP8, ROWV[:, 1536:1544], AF.Exp, [bROWV], [bSM], scale=-1.0)
            ACTV(TMP8, TMP8, AF.Ln, [bSM], [bSM], bias=1.0)
            TS("dve", LG, TMP8, -1.0, None, ALU.mult, None, [bSM], [bSM])
            for hd in range(8):
                ACTV(KCOL[:, hd:hd + 1], LG[:, hd:hd + 1], AF.Exp, [bSM, bMIXC], [bSM], scale=(c_127mp if hd < 4 else c_p))
            ACTV(DEC128, LG, AF.Exp, [bSM], [bSM], scale=128.0)
            ACTV(D2048M1, LG, AF.Exp, [bSM], [bSM], scale=2048.0)
            TS("dve", D2048M1, D2048M1, -1.0, None, ALU.add, None, [bSM], [bSM])

            R5, br5l, r5_rec = AR.alloc([1024], BF16, name="R5"); bR5 = br5l[0]
            dg, bdgl, dg_rec = AR.alloc([128], BF16, name="dg")
            for m in range(KD):
                TS("dve", dg, ident, MG[:, 1, m, 0:1], None, ALU.mult, None, [bCB, bMG], [bdgl[0]])
                MM(PS[6 + m // 4][:, (m % 4) * 128:(m % 4 + 1) * 128], ones, dg, True, True, [bCB, bdgl[0]], [bPS[6 + m // 4]])
            CP("act", R5[:, 0:512], PS[6][:], [bPS[6]], [bR5])
            CP("act", R5[:, 512:1024], PS[7][:], [bPS[7]], [bR5])
            AR.free(dg_rec)

            RC, brc, rc_rec = AR.alloc([3, 4, 128], F32, name="RCONST"); bRC = brc[0]
            CFTAB, CBTAB, DMASK = RC[:, 0], RC[:, 1], RC[:, 2]
            tmpa, tmpab, tmpa_rec = AR.alloc([128], F32, name="tmpa")
            for hh in range(4):
                ACTV(CFTAB[:, hh, :], ROWIP1, AF.Exp, [bSM, bMIXC], [bRC], scale=LG[:, hh:hh + 1], bias=LNQS)
                ACTV(CBTAB[:, hh, :], ROW128MI, AF.Exp, [bSM, bMIXC], [bRC], scale=LG[:, 4 + hh:5 + hh], bias=LNQS)
                ACTV(DMASK[:, hh, :], DPOS, AF.Exp, [bSM, bMIXC], [bRC], scale=LG[:, hh:hh + 1], bias=LNQS)
                TT("dve", DMASK[:, hh, :], DMASK[:, hh, :], UST, ALU.mult, [bRC, bMIXC], [bRC])
                ACTV(tmpa, DNEG, AF.Exp, [bSM, bMIXC], [tmpab[0]], scale=LG[:, 4 + hh:5 + hh], bias=LNQS)
                TT("dve", tmpa, tmpa, LST, ALU.mult, [tmpab[0], bMIXC], [tmpab[0]])
                TT("dve", DMASK[:, hh, :], DMASK[:, hh, :], tmpa, ALU.add, [bRC, tmpab[0]], [bRC])
                TT("dve", DMASK[:, hh, :], DMASK[:, hh, :], I2, ALU.add, [bRC, bMIXC], [bRC])
            AR.free(tmpa_rec)
            KF = bc(KCOL[:, 0:4], 2, [128, 4, 128]); KBt = bc(KCOL[:, 4:8], 2, [128, 4, 128])
            DECF = DEC128[:, 0:4]; DECB = DEC128[:, 4:8]

            AF_, bAFl, af_rec = AR.alloc([18, 4, 128], BF16, nbufs=18, name="AF")
            AB_, bABl, ab_rec = AR.alloc([18, 4, 128], BF16, nbufs=18, name="AB")

            def rope_ops(src_ps, bsrc, H, Cx, Sgx, bropet, t1, bt1, t2, bt2, dst, bdst, dst_eng):
                TT("dve", t1, src_ps, Cx, ALU.mult, [bsrc, bropet], [bt1])
                s4 = src_ps.rearrange("p (g b c) -> p g b c", b=2, c=32)
                t4 = t2.rearrange("p (g b c) -> p g b c", b=2, c=32)
                g4 = Sgx.rearrange("p (g b c) -> p g b c", b=2, c=32)
                for half in range(2):
                    TT("dve", t4[:, :, half, :], s4[:, :, 1 - half, :], g4[:, :, half, :], ALU.mult, [bsrc, bropet], [bt2])
                TT(dst_eng, dst, t1, t2, ALU.add, [bt1, bt2], [bdst])

            wt, wtb, wt_rec = AR.alloc([KD, 1024], BF16, name="wP1R")
            DMA("pool", wt, w_in[:, 512:1536].rearrange("(k p) c -> p k c", p=128), [], [wtb[0]])
            tb = []
            for i in range(2):
                d = {}
                for nm, shp, dt in (("rp", [1024], F32), ("t1", [512], F32), ("t2", [512], F32), ("kr", [512], F32),
                                    ("kf", [4, 128], BF16), ("kb", [4, 128], BF16), ("vb", [4, 128], BF16)):
                    d[nm] = AR.alloc(shp, dt, name=nm + str(i))
                tb.append(d)
            def p1rA(ci):
                kind, t = TILES18[ci]
                d = tb[ci % 2]
                pa, pb = PS[(ci % 2) * 2], PS[(ci % 2) * 2 + 1]
                bpa, bpb = bPS[(ci % 2) * 2], bPS[(ci % 2) * 2 + 1]
                for k in range(KD):
                    hs, hbk = hsrc(kind, t, k)
                    MM(pa[:], hs, wt[:, k, 0:512], k == 0, k == KD - 1, [hbk, wtb[0]], [bpa])
                for k in range(KD):
                    hs, hbk = hsrc(kind, t, k)
                    MM(pb[:], hs, wt[:, k, 512:1024], k == 0, k == KD - 1, [hbk, wtb[0]], [bpb])
                pa4 = pa[:].rearrange("p (h d) -> p h d", h=4)
                kf, kfb, _ = d["kf"]; kb_, kbb, _ = d["kb"]; vb, vbb, _ = d["vb"]
                if kind == "l":
                    rp, rpb, _ = d["rp"]
                    DMA("sp", rp, rope[t], [], [rpb[0]])
                    t1, t1b, _ = d["t1"]; t2, t2b, _ = d["t2"]; kr, krb, _ = d["kr"]
                    rope_ops(pa[:], bpa, 4, rp[:, 0:512], rp[:, 512:1024], rpb[0], t1, t1b[0], t2, t2b[0], kr, krb[0], "dve")
                    kr3 = kr.rearrange("p (h d) -> p h d", h=4)
                    TT("pool", kf, kr3, KF, ALU.mult, [krb[0], bSM], [kfb[0]])
                    TT("dve", kb_, kr3, KBt, ALU.mult, [krb[0], bSM], [kbb[0]])
                else:
                    TT("dve", kf, pa4, KF, ALU.mult, [bpa, bSM], [kfb[0]])
                    TT("dve", kb_, pa4, KBt, ALU.mult, [bpa, bSM], [kbb[0]])
                CP("act", vb, pb[:].rearrange("p (h d) -> p h d", h=4), [bpb], [vbb[0]])

            def p1rB(ci):
                d = tb[ci % 2]
                kf, kfb, _ = d["kf"]; kb_, kbb, _ = d["kb"]; vb, vbb, _ = d["vb"]
                for hh in range(4):
                    MM(PS[4][:, hh * 128:(hh + 1) * 128], kf[:, hh, :], vb[:, hh, :], True, True, [kfb[0], vbb[0]], [bPS[4]])
                for hh in range(4):
                    MM(PS[5][:, hh * 128:(hh + 1) * 128], kb_[:, hh, :], vb[:, hh, :], True, True, [kbb[0], vbb[0]], [bPS[5]])
                CP("act", AF_[:, ci], PS[4][:].rearrange("p (h d) -> p h d", h=4), [bPS[4]], [bAFl[ci]])
                CP("act", AB_[:, ci], PS[5][:].rearrange("p (h d) -> p h d", h=4), [bPS[5]], [bABl[ci]])

            p1rA(0)
            for ci in range(18):
                if ci + 1 < 18:
                    p1rA(ci + 1)
                p1rB(ci)
            for d in tb:
                for v in d.values():
                    AR.free(v[2])
            AR.free(wt_rec)

            def recur(eng, A, bA, order, dec, bdec, init, store, nm):
                pp = [AR.alloc([4, 128], F32, name=nm + "R%d" % i) for i in range(2)]
                cur = init
                n = 0
                for c in order:
                    nx = pp[n % 2]
                    n += 1
                    if cur is None:
                        CP(eng, nx[0], A[:, c], [bA[c]], [nx[1][0]])
                    else:
                        for hh in range(4):
                            STT(eng, nx[0][:, hh, :], cur[0][:, hh, :], dec[:, hh:hh + 1], A[:, c, hh, :], ALU.mult, ALU.add,
                                [cur[1], bdec, bA[c]], [nx[1][0]])
                        if store:
                            CP("act", A[:, c], cur[0], [cur[1]], [bA[c]])
                    cur = (nx[0], nx[1][0])
                return cur, [p[2] for p in pp]

            LATF = list(range(2, 18)); LATB = list(range(17, 1, -1))
            (Ff, bFf), recs1 = recur("dve", AF_, bAFl, LATF, DECF, bSM, None, False, "ff")
            (Fb, bFb), recs2 = recur("dve", AB_, bABl, LATB, DECB, bSM, None, False, "fb")
            (Cf, bCf), recs3 = recur("dve", AF_, bAFl, [0, 1], DECF, bSM, None, False, "cf")
            (Cb, bCb), recs4 = recur("dve", AB_, bABl, [1, 0], DECB, bSM, None, False, "cb")
            bcc_in, bcc_out = Buf("ccRin"), Buf("ccRout")
            DMA("sp", ccR_in[:, 0:512], Ff.rearrange("p h d -> p (h d)"), [bFf], [bcc_in])
            DMA("sp", ccR_in[:, 512:1024], Fb.rearrange("p h d -> p (h d)"), [bFb], [bcc_in])
            S.op("pool", lambda e: e.collective_compute("AllGather", ALU.bypass, replica_groups=RG, ins=[ccR_in], outs=[ccR_out]),
                 [bcc_in], [bcc_out], kind="cc")
            GR, bgr, gr_rec = AR.alloc([4, 1024], F32, name="GR")
            DMA("sp", GR, ccR_out.rearrange("(r p) c -> p r c", p=128), [bcc_out], [bgr[0]])
            av, bav, av_rec = AR.alloc([4], F32, name="avec")
            for (Sx, bSx, off, ms, rs_) in ((Cf, bCf, 0, 0, (0, 1, 2)), (Cb, bCb, 512, 4, (3, 2, 1))):
                eng = "dve"
                for rr in rs_:
                    mcol = RMK[:, ms + rr:ms + rr + 1]
                    TS(eng, av, D2048M1[:, (0 if off == 0 else 4):(4 if off == 0 else 8)], mcol, 1.0, ALU.mult, ALU.add,
                       [bSM, bRMK], [bav[0]])
                    TT(eng, Sx, Sx, bc(av, 2, [128, 4, 128]), ALU.mult, [bSx, bav[0]], [bSx])
                    STT(eng, Sx, GR[:, rr, off:off + 512].rearrange("p (h d) -> p h d", h=4), mcol, Sx, ALU.mult, ALU.add,
                        [bgr[0], bRMK, bSx], [bSx])
            _, recs5 = recur("dve", AF_, bAFl, LATF, DECF, bSM, (Cf, bCf), True, "pf")
            _, recs6 = recur("dve", AB_, bABl, LATB, DECB, bSM, (Cb, bCb), True, "pb")
            for rl in (recs1, recs2, recs3, recs4, recs5, recs6):
                for r_ in rl:
                    AR.free(r_)
            AR.free(gr_rec); AR.free(av_rec)

            def tail1(o_ps, bo, H, center, normw, gate_sb, bgate, d):
                osb, bosb, _ = d["osb"]; sq, bsq, _ = d["sq"]; st, bst, _ = d["st"]; mg, bmg, _ = d["mg"]
                CP("act", osb, o_ps, [bo], [bosb[0]])
                TT("dve", sq, osb, osb, ALU.mult, [bosb[0]], [bsq[0]])
                s1 = st[:, 0:H]; s2 = st[:, 4:4 + H]; mean = st[:, 8:8 + H]; rstd = st[:, 12:12 + H]; m2 = st[:, 16:16 + H]
                RED("dve", s2, sq, [bsq[0]], [bst[0]])
                if center:
                    RED("dve", s1, osb, [bosb[0]], [bst[0]])
                    TS("dve", mean, s1, 1.0 / 128, None, ALU.mult, None, [bst[0]], [bst[0]])
                    TT("dve", m2, mean, mean, ALU.mult, [bst[0]], [bst[0]])
                    STT("dve", s2, s2, 1.0 / 128, m2, ALU.mult, ALU.subtract, [bst[0]], [bst[0]])
                    ACTV(rstd, s2, AF.Ln, [bst[0]], [bst[0]], bias=EPS)
                else:
                    ACTV(rstd, s2, AF.Ln, [bst[0]], [bst[0]], bias=EPS, scale=1.0 / 128)
                ACTV(rstd, rstd, AF.Exp, [bst[0]], [bst[0]], scale=-0.5)
                nmr = st[:, 20:20 + H]
                if center:
                    STT("dve", nmr, mean, -1.0, rstd, ALU.mult, ALU.mult, [bst[0]], [bst[0]])
                for hh in range(H):
                    if center:
                        ACTV(osb[:, hh, :], osb[:, hh, :], AF.Identity, [bosb[0], bst[0]], [bosb[0]],
                             scale=rstd[:, hh:hh + 1], bias=nmr[:, hh:hh + 1])
                    else:
                        ACTV(osb[:, hh, :], osb[:, hh, :], AF.Identity, [bosb[0], bst[0]], [bosb[0]], scale=rstd[:, hh:hh + 1])
                TT("dve", mg, osb, gate_sb, ALU.mult, [bosb[0], bgate], [bmg[0]])

            def gate_ops(g_ps, bg, normw, d):
                ge, geb, _ = d["ge"]; gt, gtb, _ = d["gt"]
                ACTV(ge, g_ps, AF.Exp, [bg], [geb[0]], scale=-1.0)
                ACTV(ge, ge, AF.Ln, [geb[0]], [geb[0]], bias=1.0)
                ACTV(ge, ge, AF.Exp, [geb[0]], [geb[0]], scale=-1.0)
                TT("dve", ge, g_ps, ge, ALU.mult, [bg, geb[0]], [geb[0]])
                TT("pool", gt, ge, normw, ALU.mult, [geb[0], bROWV], [gtb[0]])

            def tail2(H, wo, bwo, t, d, ptr, bptr, py0):
                mg, bmg, _ = d["mg"]; mT, bmT, _ = d["mT"]
                for hh in range(H):
                    TR(ptr[:, hh, :], mg[:, hh, :], [bmg[0]], [bptr])
                CP("act", mT, ptr[:, 0:H, :], [bptr], [bmT[0]])
                y = [PS[py0][:].rearrange("p (m t) -> p m t", m=4), PS[py0 + 1][:].rearrange("p (m t) -> p m t", m=4)]
                for m in range(KD):
                    for cc in range(H):
                        MM(y[m // 4][:, m % 4, :], wo[:, cc, m * 128:(m + 1) * 128], mT[:, cc, :], cc == 0, cc == H - 1,
                           [bwo, bmT[0]], [bPS[py0 + m // 4]])
                for hf in range(2):
                    dst = X[:, 4 * hf:4 * hf + 4, t * 128:(t + 1) * 128]
                    xbs = [bX[m][t] for m in range(4 * hf, 4 * hf + 4)]
                    TT("dve", dst, y[hf], dst, ALU.add, [bPS[py0 + hf]] + xbs, xbs)

            TAILBUFS = (("ge", [2, 128], F32), ("osb", [2, 128], F32), ("sq", [2, 128], F32), ("st", [24], F32), ("mg", [2, 128], BF16), ("mT", [2, 128], BF16))

            PS4b = PS[4][:].bitcast(BF16)
            for hp in range(2):
                wt, wtb, wt_rec = AR.alloc([KD, 1024], BF16, name="wP2R")
                for gi in range(4):
                    c0 = gi * 512 + hp * 256
                    DMA("pool", wt[:, :, gi * 256:(gi + 1) * 256], w_in[:, c0:c0 + 256].rearrange("(k p) c -> p k c", p=128), [], [wtb[0]])
                wo, wob, wo_rec = AR.alloc([2, D], BF16, name="woR")
                DMA("pool", wo, w_out[hp * 256:(hp + 1) * 256, :].rearrange("(c p) d -> p c d", p=128), [], [wob[0]])
                TT("pool", wo, wo, bc(R5, 1, [128, 2, D]), ALU.mult, [wob[0], bR5], [wob[0]])
                t1s = AR.alloc([512], F32, name="t1s"); t2s = AR.alloc([512], F32, name="t2s")
                tb = []
                for i in range(2):
                    d = {"t1": t1s, "t2": t2s} if i == 0 else {}
                    for nm, shp, dt in (("rp", [1024], F32), ("qr", [512], BF16), ("vb", [2, 128], BF16), ("gt", [2, 128], BF16),
                                        ("qT", [2, 128], BF16), ("qfT", [2, 128], BF16), ("qbT", [2, 128], BF16), ("kT", [2, 128], BF16),
                                        ("pT", [2, 128], BF16)) + TAILBUFS:
                        d[nm] = AR.alloc(shp, dt, name=nm + str(i))
                    tb.append(d)
                def stA(t, hp=hp, wt=wt, wtb=wtb, tb=tb):
                    d = tb[t % 2]
                    pa, pb = PS[(t % 2) * 2], PS[(t % 2) * 2 + 1]
                    bpa, bpb = bPS[(t % 2) * 2], bPS[(t % 2) * 2 + 1]
                    for k in range(KD):
                        hs, hbk = hsrc("l", t, k)
                        MM(pa[:], hs, wt[:, k, 0:512], k == 0, k == KD - 1, [hbk, wtb[0]], [bpa])
                    for k in range(KD):
                        hs, hbk = hsrc("l", t, k)
                        MM(pb[:], hs, wt[:, k, 512:1024], k == 0, k == KD - 1, [hbk, wtb[0]], [bpb])
                    rp, rpb, _ = d["rp"]
                    DMA("sp", rp, rope[t], [], [rpb[0]])
                    t1, t1b, _ = t1s; t2, t2b, _ = t2s
                    qr, qrb, _ = d["qr"]; vb, vbb, _ = d["vb"]
                    rope_ops(pa[:], bpa, 4, rp[:, 0:512], rp[:, 512:1024], rpb[0], t1, t1b[0], t2, t2b[0], qr, qrb[0], "pool")

                def stA2(t, hp=hp, tb=tb):
                    d = tb[t % 2]
                    pb, bpb = PS[(t % 2) * 2 + 1], bPS[(t % 2) * 2 + 1]
                    vb, vbb, _ = d["vb"]
                    CP("act", vb, pb[:, 0:256].rearrange("p (h d) -> p h d", h=2), [bpb], [vbb[0]])
                    gate_ops(pb[:, 256:512].rearrange("p (h d) -> p h d", h=2), bpb,
                             ROWV[:, hp * 256:(hp + 1) * 256].rearrange("p (h e) -> p h e", h=2), d)

                def stB(t, hp=hp, wo=wo, wob=wob, tb=tb):
                    ci = t + 2
                    d = tb[t % 2]
                    qr, qrb, _ = d["qr"]
                    half = (t % 2) * 512
                    ptq = PS4b[:, half:half + 512].rearrange("p (a t) -> p a t", a=4)
                    for a_ in range(4):
                        TR(ptq[:, a_, :], qr[:, a_ * 128:(a_ + 1) * 128], [qrb[0]], [bPS[4]])
                    qT, qTb, _ = d["qT"]; qfT, qfTb, _ = d["qfT"]; qbT, qbTb, _ = d["qbT"]; kT, kTb, _ = d["kT"]
                    CP("act", qT, ptq[:, 0:2, :], [bPS[4]], [qTb[0]])
                    TT("dve", qfT, ptq[:, 0:2, :], CFTAB[:, 2 * hp:2 * hp + 2, :], ALU.mult, [bPS[4], bRC], [qfTb[0]])
                    TT("dve", qbT, ptq[:, 0:2, :], CBTAB[:, 2 * hp:2 * hp + 2, :], ALU.mult, [bPS[4], bRC], [qbTb[0]])
                    CP("act", kT, ptq[:, 2:4, :], [bPS[4]], [kTb[0]])
                    vb, vbb, _ = d["vb"]; gt, gtb, _ = d["gt"]; pT, pTb, _ = d["pT"]
                    sc = PS[5][:, 0:256].rearrange("p (h i) -> p h i", h=2)
                    o_ps = PS[5][:, 256:512].rearrange("p (h e) -> p h e", h=2)
                    for hh in range(2):
                        MM(sc[:, hh, :], kT[:, hh, :], qT[:, hh, :], True, True, [kTb[0], qTb[0]], [bPS[5]])
                    TT("dve", pT, sc, DMASK[:, 2 * hp:2 * hp + 2, :], ALU.mult, [bPS[5], bRC], [pTb[0]])
                    for hh in range(2):
                        hd = 2 * hp + hh
                        MM(o_ps[:, hh, :], pT[:, hh, :], vb[:, hh, :], True, False, [pTb[0], vbb[0]], [bPS[5]])
                        MM(o_ps[:, hh, :], qfT[:, hh, :], AF_[:, ci, hd, :], False, False, [qfTb[0], bAFl[ci]], [bPS[5]])
                        MM(o_ps[:, hh, :], qbT[:, hh, :], AB_[:, ci, hd, :], False, True, [qbTb[0], bABl[ci]], [bPS[5]])
                    tail1(o_ps, bPS[5], 2, True, ROWV[:, hp * 256:(hp + 1) * 256].rearrange("p (h e) -> p h e", h=2),
                          gt, gtb[0], d)

                def stC(t, wo=wo, wob=wob, tb=tb):
                    ptm = PS[7][:].bitcast(BF16)[:, 0:256].rearrange("p (a t) -> p a t", a=2)
                    tail2(2, wo, wob[0], t, tb[t % 2], ptm, bPS[7], 6)

                for s_ in range(18):
                    if s_ < 16:
                        stA(s_)
                    if s_ - 2 >= 0:
                        stC(s_ - 2)
                    if 0 <= s_ - 1 < 16:
                        stB(s_ - 1)
                    if s_ < 16:
                        stA2(s_)
                for d in tb:
                    for v in d.values():
                        AR.free(v[2])
                AR.free(wt_rec); AR.free(wo_rec)
            AR.free(af_rec); AR.free(ab_rec); AR.free(rc_rec)

            WG, bwgl, wg_rec = AR.alloc([512], BF16, name="WG"); bWG = bwgl[0]
            DMA("pool", WG[0:32, :], wg, [], [bWG])
            ET, betl, et_rec = AR.alloc([18, 4], F32, name="ETOT"); bET = betl[0]
            AG, bAGl, ag_rec = AR.alloc([18, 4, 128], BF16, nbufs=18, name="AG")
            GBIAS = ROWV[:, 1024:1536]

            def gates(z_ps, bz, b_ps, bb_, lrT_ps, blr, W, wgs, bias, d, need_tot, tot_ps, btot, etdst):
                lr, lrb, _ = d["lr"]; zb, zbb, _ = d["zb"]
                if lrT_ps is not None:
                    CP("act", lr[0:32, :], lrT_ps, [blr], [lrb[0]])
                MM(z_ps, lr[0:32, :], wgs, True, True, [lrb[0], bWG], [bz])
                TT("dve", zb, z_ps, bias, ALU.add, [bz, bROWV], [zbb[0]])
                ACTV(zb, zb, AF.Exp, [zbb[0]], [zbb[0]], scale=-1.0)
                ACTV(zb, zb, AF.Ln, [zbb[0]], [zbb[0]], bias=1.0)
                nh = W // 128
                for hh in range(nh):
                    c0 = hh * 128
                    MM(b_ps[:, c0:c0 + 64], MU, zb[:, c0:c0 + 64], True, True, [zbb[0], bMIXC], [bb_])
                    MM(b_ps[:, c0 + 64:c0 + 128], ML, zb[:, c0 + 64:c0 + 128], True, True, [zbb[0], bMIXC], [bb_])
                if need_tot:
                    for hh in range(nh):
                        MM(tot_ps[:, hh:hh + 1], zb[:, hh * 128:(hh + 1) * 128], negcol, True, True, [zbb[0], bMIXC], [btot])
                    ACTV(etdst, tot_ps[:, 0:nh], AF.Exp, [btot], [bET])

            wt, wtb, wt_rec = AR.alloc([KD, 800], BF16, name="wP1G")
            DMA("pool", wt[:, :, 0:768], w_in[:, 2304:3072].rearrange("(k p) c -> p k c", p=128), [], [wtb[0]])
            DMA("pool", wt[:, :, 768:800], w_in[:, 3584:3616].rearrange("(k p) c -> p k c", p=128), [], [wtb[0]])
            tb = []
            for i in range(2):
                d = {}
                for nm, shp, dt in (("lr", [128], BF16), ("zb", [512], F32), ("en", [512], F32), ("kt", [512], BF16), ("vb", [4, 128], BF16)):
                    d[nm] = AR.alloc(shp, dt, name=nm + str(i))
                tb.append(d)
            def p1gA1(ci):
                kind, t = TILES18[ci]
                d = tb[ci % 2]
                pa, pb = PS[(ci % 2) * 2], PS[(ci % 2) * 2 + 1]
                bpa, bpb = bPS[(ci % 2) * 2], bPS[(ci % 2) * 2 + 1]
                for k in range(KD):
                    hs, hbk = hsrc(kind, t, k)
                    MM(pa[:, 0:256], hs, wt[:, k, 0:256], k == 0, k == KD - 1, [hbk, wtb[0]], [bpa])
                for k in range(KD):
                    hs, hbk = hsrc(kind, t, k)
                    MM(pb[:], hs, wt[:, k, 256:768], k == 0, k == KD - 1, [hbk, wtb[0]], [bpb])
                for k in range(KD):
                    hs, hbk = hsrc(kind, t, k)
                    MM(pa[0:32, 256:384], wt[:, k, 768:800], hs, k == 0, k == KD - 1, [hbk, wtb[0]], [bpa])
                lr, lrb, _ = d["lr"]
                CP("act", lr[0:32, :], pa[0:32, 256:384], [bpa], [lrb[0]])

            def p1gA2(ci):
                d = tb[ci % 2]
                pa, pb = PS[(ci % 2) * 2], PS[(ci % 2) * 2 + 1]
                bpa, bpb = bPS[(ci % 2) * 2], bPS[(ci % 2) * 2 + 1]
                gates(PS[4][:], bPS[4], PS[5][:], bPS[5], None, None, 512, WG[0:32, :], GBIAS, d, True,
                      pa[:, 384:388], bpa, ET[:, ci, :])
                en, enb, _ = d["en"]; kt, ktb, _ = d["kt"]; vb, vbb, _ = d["vb"]
                ACTV(en, PS[5][:], AF.Exp, [bPS[5]], [enb[0]], scale=-1.0)
                k3 = pa[:, 0:256].rearrange("p (h d) -> p h d", h=4)
                kt4 = kt.rearrange("p (h a d) -> p h a d", h=4, a=2); en4 = en.rearrange("p (h a d) -> p h a d", h=4, a=2)
                for a_ in range(2):
                    TT("dve", kt4[:, :, a_, :], k3, en4[:, :, a_, :], ALU.mult, [bpa, enb[0]], [ktb[0]])
                CP("act", vb, pb[:].rearrange("p (h e) -> p h e", h=4), [bpb], [vbb[0]])

            def p1gB(ci):
                d = tb[ci % 2]
                kt, ktb, _ = d["kt"]; vb, vbb, _ = d["vb"]
                for hh in range(4):
                    MM(PS[6][:, hh * 128:(hh + 1) * 128], kt[:, hh * 128:(hh + 1) * 128], vb[:, hh, :], True, True,
                       [ktb[0], vbb[0]], [bPS[6]])
                CP("dve", AG[:, ci], PS[6][:].rearrange("p (h e) -> p h e", h=4), [bPS[6]], [bAGl[ci]])

            for it in range(20):
                if it < 18:
                    p1gA1(it)
                if 0 <= it - 1 < 18:
                    p1gA2(it - 1)
                if 0 <= it - 2 < 18:
                    p1gB(it - 2)
            for d in tb:
                for v in d.values():
                    AR.free(v[2])
            AR.free(wt_rec)

            def grecur(eng, lo, order, init, store, nm):
                pp = [AR.alloc([4, 128], F32, name=nm + "R%d" % i) for i in range(2)]
                sl = slice(lo, lo + 64)
                cur = init
                n = 0
                for c in order:
                    nx = pp[n % 2]
                    n += 1
                    Ec = bc(ET[sl, c, :], 2, [64, 4, 128])
                    if cur is None:
                        TT(eng, nx[0][sl], AG[sl, c], Ec, ALU.mult, [bAGl[c], bET], [nx[1][0]])
                    else:
                        TT(eng, nx[0][sl], cur[0][sl], AG[sl, c], ALU.add, [cur[1], bAGl[c]], [nx[1][0]])
                        TT(eng, nx[0][sl], nx[0][sl], Ec, ALU.mult, [nx[1][0], bET], [nx[1][0]])
                        if store:
                            CP("act", AG[sl, c], cur[0][sl], [cur[1]], [bAGl[c]])
                    cur = (nx[0], nx[1][0])
                return cur, [p[2] for p in pp]

            (Gf, bGf), g1 = grecur("dve", 0, LATF, None, False, "gf")
            (Gb, bGb), g2 = grecur("pool", 64, LATB, None, False, "gb")
            (GCf, bGCf), g3 = grecur("dve", 0, [0, 1], None, False, "gcf")
            (GCb, bGCb), g4 = grecur("pool", 64, [1, 0], None, False, "gcb")
            DC, bdcl, dc_rec = AR.alloc([4], F32, name="DCORE")
            CP("dve", DC, ET[:, 2, :], [bET], [bdcl[0]])
            for c in range(3, 18):
                TT("dve", DC, DC, ET[:, c, :], ALU.mult, [bdcl[0], bET], [bdcl[0]])
            bg_in, bg_out = Buf("ccGin"), Buf("ccGout")
            DMA("sp", ccG_in[0:64, 0:512], Gf[0:64].rearrange("p h d -> p (h d)"), [bGf], [bg_in])
            DMA("sp", ccG_in[64:128, 0:512], Gb[64:128].rearrange("p h d -> p (h d)"), [bGb], [bg_in])
            DMA("sp", ccG_in[:, 512:516], DC, [bdcl[0]], [bg_in])
            S.op("pool", lambda e: e.collective_compute("AllGather", ALU.bypass, replica_groups=RG, ins=[ccG_in], outs=[ccG_out]),
                 [bg_in], [bg_out], kind="cc")
            GG, bggl, gg_rec = AR.alloc([4, 516], F32, name="GG")
            DMA("sp", GG, ccG_out.rearrange("(r p) c -> p r c", p=128), [bg_out], [bggl[0]])
            av, bav, av_rec = AR.alloc([4], F32, name="gavec")
            for (Sx, bSx, lo, ms, rs_, eng) in ((GCf, bGCf, 0, 0, (0, 1, 2), "dve"), (GCb, bGCb, 64, 4, (3, 2, 1), "dve")):
                sl = slice(lo, lo + 64)
                for rr in rs_:
                    mcol = RMK[sl, ms + rr:ms + rr + 1]
                    TS(eng, av[sl], GG[sl, rr, 512:516], -1.0, mcol, ALU.add, ALU.mult, [bggl[0], bRMK], [bav[0]])
                    TS(eng, av[sl], av[sl], 1.0, None, ALU.add, None, [bav[0]], [bav[0]])
                    TT(eng, Sx[sl], Sx[sl], bc(av[sl], 2, [64, 4, 128]), ALU.mult, [bSx, bav[0]], [bSx])
                    STT(eng, Sx[sl], GG[sl, rr, 0:512].rearrange("p (h d) -> p h d", h=4), mcol, Sx[sl], ALU.mult, ALU.add,
                        [bggl[0], bRMK, bSx], [bSx])
            _, g5 = grecur("dve", 0, LATF, (GCf, bGCf), True, "gpf")
            _, g6 = grecur("pool", 64, LATB, (GCb, bGCb), True, "gpb")
            for rl in (g1, g2, g3, g4, g5, g6):
                for r_ in rl:
                    AR.free(r_)
            AR.free(gg_rec); AR.free(av_rec); AR.free(dc_rec); AR.free(h2c_rec)

            for hp in range(2):
                wt, wtb, wt_rec = AR.alloc([KD, 800], BF16, name="wP2G")
                for (o0, c0, wd) in ((0, 2048 + hp * 128, 128), (128, 2304 + hp * 128, 128), (256, 2560 + hp * 256, 256),
                                     (512, 3072 + hp * 256, 256), (768, 3584, 32)):
                    DMA("pool", wt[:, :, o0:o0 + wd], w_in[:, c0:c0 + wd].rearrange("(k p) c -> p k c", p=128), [], [wtb[0]])
                wo, wob, wo_rec = AR.alloc([2, D], BF16, name="woG")
                DMA("pool", wo, w_out[512 + hp * 256:512 + (hp + 1) * 256, :].rearrange("(c p) d -> p c d", p=128), [], [wob[0]])
                TT("pool", wo, wo, bc(R5, 1, [128, 2, D]), ALU.mult, [wob[0], bR5], [wob[0]])
                tb = []
                for i in range(2):
                    d = {}
                    for nm, shp, dt in (("lr", [128], BF16), ("zb", [256], F32), ("ep", [256], F32), ("en", [256], F32),
                                        ("qt", [2, 128], BF16), ("kt", [2, 128], BF16), ("vb", [2, 128], BF16), ("gt", [2, 128], BF16),
                                        ("qkT", [4, 128], BF16), ("pf", [2, 128], BF16), ("pb", [2, 128], BF16)) + TAILBUFS:
                        d[nm] = AR.alloc(shp, dt, name=nm + str(i))
                    tb.append(d)
                def stA(t, hp=hp, wt=wt, wtb=wtb, tb=tb):
                    d = tb[t % 2]
                    pa, pb = PS[(t % 2) * 2], PS[(t % 2) * 2 + 1]
                    bpa, bpb = bPS[(t % 2) * 2], bPS[(t % 2) * 2 + 1]
                    for k in range(KD):
                        hs, hbk = hsrc("l", t, k)
                        MM(pa[:], hs, wt[:, k, 0:512], k == 0, k == KD - 1, [hbk, wtb[0]], [bpa])
                    for k in range(KD):
                        hs, hbk = hsrc("l", t, k)
                        MM(pb[:, 0:256], hs, wt[:, k, 512:768], k == 0, k == KD - 1, [hbk, wtb[0]], [bpb])
                    for k in range(KD):
                        hs, hbk = hsrc("l", t, k)
                        MM(pb[0:32, 256:384], wt[:, k, 768:800], hs, k == 0, k == KD - 1, [hbk, wtb[0]], [bpb])
                    z_ps = PS[4][:, 0:256]; b_ps = PS[4][:, 256:512]
                    gates(z_ps, bPS[4], b_ps, bPS[4], pb[0:32, 256:384], bpb, 256, WG[0:32, hp * 256:(hp + 1) * 256],
                          GBIAS[:, hp * 256:(hp + 1) * 256], d, False, None, None, None)
                    ep, epb, _ = d["ep"]; en, enb, _ = d["en"]
                    ACTV(ep, b_ps, AF.Exp, [bPS[4]], [epb[0]])
                    ACTV(en, b_ps, AF.Exp, [bPS[4]], [enb[0]], scale=-1.0)
                    qt, qtb, _ = d["qt"]; kt, ktb, _ = d["kt"]; vb, vbb, _ = d["vb"]; gt, gtb, _ = d["gt"]
                    q3 = pa[:, 0:128].rearrange("p (h d) -> p h d", h=2)
                    k3 = pa[:, 128:256].rearrange("p (h d) -> p h d", h=2)
                    qt4 = qt.rearrange("p h (a d) -> p h a d", a=2); kt4 = kt.rearrange("p h (a d) -> p h a d", a=2)
                    ep4 = ep.rearrange("p (h a d) -> p h a d", h=2, a=2); en4 = en.rearrange("p (h a d) -> p h a d", h=2, a=2)
                    for a_ in range(2):
                        STT("dve", qt4[:, :, a_, :], q3, 0.125, ep4[:, :, a_, :], ALU.mult, ALU.mult, [bpa, epb[0]], [qtb[0]])
                        TT("dve", kt4[:, :, a_, :], k3, en4[:, :, a_, :], ALU.mult, [bpa, enb[0]], [ktb[0]])

                def stA2(t, hp=hp, tb=tb):
                    d = tb[t % 2]
                    pa, pb = PS[(t % 2) * 2], PS[(t % 2) * 2 + 1]
                    bpa, bpb = bPS[(t % 2) * 2], bPS[(t % 2) * 2 + 1]
                    vb, vbb, _ = d["vb"]
                    CP("act", vb, pa[:, 256:512].rearrange("p (h e) -> p h e", h=2), [bpa], [vbb[0]])
                    gate_ops(pb[:, 0:256].rearrange("p (h e) -> p h e", h=2), bpb,
                             ROWV[:, 512 + hp * 256:512 + (hp + 1) * 256].rearrange("p (h e) -> p h e", h=2), d)

                def stB(t, hp=hp, wo=wo, wob=wob, tb=tb):
                    ci = t + 2
                    d = tb[t % 2]
                    qt, qtb, _ = d["qt"]; kt, ktb, _ = d["kt"]
                    ptq = PS[5][:].bitcast(BF16)[:, 0:512].rearrange("p (a t) -> p a t", a=4)
                    for hh in range(2):
                        TR(ptq[:, hh, :], qt[:, hh, :], [qtb[0]], [bPS[5]])
                        TR(ptq[:, 2 + hh, :], kt[:, hh, :], [ktb[0]], [bPS[5]])
                    qkT, qkTb, _ = d["qkT"]
                    CP("act", qkT, ptq, [bPS[5]], [qkTb[0]])
                    vb, vbb, _ = d["vb"]; gt, gtb, _ = d["gt"]
                    scf = PS[6][:, 0:256].rearrange("p (h i) -> p h i", h=2)
                    scb = PS[7][:, 0:256].rearrange("p (h i) -> p h i", h=2)
                    for hh in range(2):
                        MM(scf[:, hh, :], qkT[0:64, 2 + hh, :], qkT[0:64, hh, :], True, True, [qkTb[0]], [bPS[6]])
                    for hh in range(2):
                        MM(scb[:, hh, :], qkT[64:128, 2 + hh, :], qkT[64:128, hh, :], True, True, [qkTb[0]], [bPS[7]])
                    pf, pfb, _ = d["pf"]; pbt, pbb, _ = d["pb"]
                    TT("dve", pf, scf, bc(TRIU, 1, [128, 2, 128]), ALU.mult, [bPS[6], bMIXC], [pfb[0]])
                    TT("dve", pbt, scb, bc(TRIL, 1, [128, 2, 128]), ALU.mult, [bPS[7], bMIXC], [pbb[0]])
                    o_ps = PS[6][:, 256:512].rearrange("p (h e) -> p h e", h=2)
                    for hh in range(2):
                        hd = 2 * hp + hh
                        MM(o_ps[:, hh, :], pf[:, hh, :], vb[:, hh, :], True, False, [pfb[0], vbb[0]], [bPS[6]])
                        MM(o_ps[:, hh, :], pbt[:, hh, :], vb[:, hh, :], False, False, [pbb[0], vbb[0]], [bPS[6]])
                        MM(o_ps[:, hh, :], qkT[:, hh, :], AG[:, ci, hd, :], False, True, [qkTb[0], bAGl[ci]], [bPS[6]])
                    tail1(o_ps, bPS[6], 2, False, ROWV[:, 512 + hp * 256:512 + (hp + 1) * 256].rearrange("p (h e) -> p h e", h=2),
                          gt, gtb[0], d)

                def stC(t, wo=wo, wob=wob, tb=tb):
                    ptm = PS[7][:].bitcast(BF16)[:, 512:768].rearrange("p (a t) -> p a t", a=2)
                    tail2(2, wo, wob[0], t, tb[t % 2], ptm, bPS[7], 6)

                for s_ in range(18):
                    if s_ < 16:
                        stA(s_)
                    if s_ - 2 >= 0:
                        stC(s_ - 2)
                    if 0 <= s_ - 1 < 16:
                        stB(s_ - 1)
                    if s_ < 16:
                        stA2(s_)
                for d in tb:
                    for v in d.values():
                        AR.free(v[2])
                AR.free(wt_rec); AR.free(wo_rec)
            for r_ in (wg_rec, et_rec, ag_rec, mixc_rec, rowv_rec, rmk_rec, sm_rec, h2_rec, r5_rec):
                AR.free(r_)

        def final_norm():
            sq, sqb, sq_rec = AR.alloc([KD, 512], BF16, name="fsq")
            rs, rsb, rs_rec = AR.alloc([512], F32, name="frstd")
            for T in range(4):
                src = X[:, :, T * 512:(T + 1) * 512]
                allb = [b for k in range(KD) for b in xb(k, T * 512, T * 512 + 512)]
                ACTV(sq, src, AF.Square, allb, [sqb[0]])
                for k in range(KD):
                    MM(PS[7][:], ones, sq[:, k, :], k == 0, k == KD - 1, [sqb[0], bCB], [bPS[7]])
                ACTV(rs, PS[7][:], AF.Ln, [bPS[7]], [rsb[0]], bias=EPS, scale=1.0 / D)
                ACTV(rs, rs, AF.Exp, [rsb[0]], [rsb[0]], scale=-0.5)
                for k in range(KD):
                    STT("dve", src[:, k, :], src[:, k, :], VEC[:, 24 + k:25 + k], rs, ALU.mult, ALU.mult,
                        xb(k, T * 512, T * 512 + 512) + [bVEC, rsb[0]], xb(k, T * 512, T * 512 + 512))

        ffn(0, f1w1, f1w3, f1w2, True, hook=lambda G: ada_mod(3 + G) if G < 6 else None)
        ada_finish(3, 9, [1, 2])
        for a in aw:
            AR.free(a[2])
        if stage in ("mix", "ffn2", "full"):
            mixing()
        else:
            AR.free(xc_rec)
        if stage in ("ffn2", "full"):
            ffn(2, f2w1, f2w3, f2w2, False)
        if stage == "full":
            final_norm()

        outs = []
        for T in range(4):
            outs.append(DMA("sp", outT[:, T * 512:(T + 1) * 512].rearrange("(k p) t -> p k t", p=128),
                            X[:, :, T * 512:(T + 1) * 512],
                            [b for k in range(KD) for b in xb(k, T * 512, T * 512 + 512)], []))
        S.emit(final_waits=outs)
    return nc


def _fm(v):
    return np.ascontiguousarray(np.asarray(v, np.float32).reshape(KD, 128).T)


def _mix_consts():
    m = np.zeros((128, NMC), np.float32)
    p = np.arange(128, dtype=np.float32)
    m[:, 0] = 127 - p; m[:, 1] = p; m[:, 2] = -1.0 / 16
    m[:, 4:132] = (p + 1)[None, :]; m[:, 132:260] = (128 - p)[None, :]
    j = p[:, None]; i = p[None, :]
    m[:, 260:388] = np.maximum(i - j, 0); m[:, 388:516] = np.maximum(j - i, 0)
    m[:, 516:644] = (i > j); m[:, 644:772] = (j > i); m[:, 772:900] = 2.0 * QS * np.eye(128)
    m[:, 900:1028] = (j <= i) * (-1.0 / 16); m[:, 1028:1156] = (j >= i) * (-1.0 / 16)
    m[:, 1156:1284] = (j <= i); m[:, 1284:1412] = (j >= i)
    return m


def _rope_tables(q):
    pos = q * NT + np.arange(NT)
    row = (pos // 64).astype(np.float64); col = (pos % 64).astype(np.float64)
    freqs = 10000.0 ** (-np.arange(32, dtype=np.float64) / 32.0)
    ar = row[:, None] * freqs[None, :]; ac = col[:, None] * freqs[None, :]
    cr, sr, cc, sc = np.cos(ar), np.sin(ar), np.cos(ac), np.sin(ac)
    C = np.concatenate([cr, cr, cc, cc], 1); Sg = np.concatenate([-sr, sr, -sc, sc], 1)
    return np.ascontiguousarray(np.concatenate([C] * 4 + [Sg] * 4, 1).reshape(16, 128, 1024).astype(np.float32))


def make_in_maps(inp):
    f32 = lambda a: np.ascontiguousarray(np.asarray(a, np.float32))
    vecs = np.zeros((128, 104), np.float32)
    vecs[:, 0:8] = _fm(inp["norm1_w"][0]); vecs[:, 8:16] = _fm(inp["norm2_w"][0])
    vecs[:, 16:24] = _fm(inp["norm3_w"][0]); vecs[:, 24:32] = _fm(inp["final_norm_w"])
    vecs[:, 32:104] = np.asarray(inp["ada_b"][0], np.float32).reshape(72, 128).T
    cb = np.zeros((128, 256), np.float32)
    cb[:, 0:128] = np.eye(128); cb[:, 128:256] = 1.0
    rowv = np.zeros((1, 1544), np.float32)
    rowv[0, 0:512] = inp["ret_norm_w"][0]; rowv[0, 512:1024] = inp["gla_norm_w"][0]
    gb = np.zeros((4, 2, 64), np.float32)
    gb[:, 0, :] = np.asarray(inp["gla_gate_b_f"][0]).reshape(4, 64); gb[:, 1, :] = np.asarray(inp["gla_gate_b_b"][0]).reshape(4, 64)
    rowv[0, 1024:1536] = gb.reshape(-1)
    rowv[0, 1536:1540] = inp["ret_decay_f"][0]; rowv[0, 1540:1544] = inp["ret_decay_b"][0]
    wgm = np.zeros((32, 4, 2, 64), np.float32)
    wgm[0:16, :, 0, :] = np.asarray(inp["gla_gate_w_f"][0]).reshape(16, 4, 64)
    wgm[16:32, :, 1, :] = np.asarray(inp["gla_gate_w_b"][0]).reshape(16, 4, 64)
    shared = {
        "vecs": vecs, "consts_bf": cb, "ada_w": f32(inp["ada_w"][0]),
        "f1w1": f32(inp["ffn1_w1"][0]), "f1w3": f32(inp["ffn1_w3"][0]), "f1w2": f32(inp["ffn1_w2"][0]),
        "f2w1": f32(inp["ffn2_w1"][0]), "f2w3": f32(inp["ffn2_w3"][0]), "f2w2": f32(inp["ffn2_w2"][0]),
        "w_in": f32(inp["w_in"][0]), "w_out": f32(inp["w_out"][0]), "rowv": rowv,
        "wg": np.ascontiguousarray(wgm.reshape(32, 512)), "mixc": _mix_consts(),
    }
    maps = []
    x = np.asarray(inp["x"], np.float32); ctx = np.asarray(inp["ctx"], np.float32)
    for r in range(NCORES):
        b, q = r // 4, r % 4
        m = dict(shared)
        m["xT"] = np.ascontiguousarray(x[b, q * NT:(q + 1) * NT, :].T)
        m["ctxT"] = np.ascontiguousarray(ctx[b].T)
        cnd = np.zeros((128, 16), np.float32)
        cnd[:, 0:8] = _fm(inp["c"][b]); cnd[:, 8:16] = _fm(inp["c_ctx"])
        m["cond"] = cnd
        m["rope"] = _rope_tables(q)
        rm = np.zeros((1, 8), np.float32)
        for rr in range(4):
            rm[0, rr] = 1.0 if rr < q else 0.0
            rm[0, 4 + rr] = 1.0 if rr > q else 0.0
        m["rmask"] = rm
        maps.append(m)
    return maps


def run(inp, stage="full", **kw):
    nc = build_program(stage, **kw)
    maps = make_in_maps(inp)
    res = run_bass_kernel_spmd(nc, maps, core_ids=list(range(NCORES)))
    out = np.empty((2, 4 * NT, D), np.float32)
    for r in range(NCORES):
        b, q = r // 4, r % 4
        out[b, q * NT:(q + 1) * NT, :] = res.results[r]["outT"].T
    return out


def kernel(**inputs):
    return run(inputs, "full")
```

```python
from contextlib import ExitStack
import numpy as np
import concourse.bass as bass
import concourse.mybir as mybir
from concourse.bass_utils import run_bass_kernel_spmd

F32 = mybir.dt.float32
BF16 = mybir.dt.bfloat16
ALU = mybir.AluOpType
AF = mybir.ActivationFunctionType

NCORES = 8
D = 1024
KD = 8
NT = 2048
NCTX = 256
DFF = 2816
NFF = 22
INC = 3616
EPS = 1e-6
ENGS = ("pe", "act", "dve", "pool", "sp")
NDMASEM = 8


class Buf:
    __slots__ = ("name", "last_w", "readers", "excl")

    def __init__(self, name="", excl=False):
        self.name = name
        self.last_w = None
        self.readers = {}
        self.excl = excl


class Op:
    __slots__ = ("eng", "fn", "deps", "marked", "sem", "val", "kind", "seq")


class Sched:
    def __init__(self, nc):
        self.nc = nc
        self.ops = {e: [] for e in ENGS}
        self.dma_count = {e: 0 for e in ENGS}
        self.dma_last = {}
        self.cc_count = 0
        self.seq = 0

    @staticmethod
    def _key(o):
        return o.eng if o.kind == "c" else o.sem

    def op(self, eng, fn, reads=(), writes=(), kind="c"):
        o = Op()
        o.eng, o.fn, o.kind = eng, fn, kind
        o.deps, o.marked, o.sem, o.val = [], False, None, 0
        o.seq = self.seq
        self.seq += 1
        deps = []
        for b in reads:
            if b.last_w is not None:
                deps.append((b.last_w, True))
            if b.excl:
                for r in b.readers.values():
                    if r.eng != eng:
                        deps.append((r, False))
        for b in writes:
            if b.last_w is not None:
                deps.append((b.last_w, False))
            for r in b.readers.values():
                deps.append((r, False))
        if kind == "dma":
            k = self.dma_count[eng]
            self.dma_count[eng] += 1
            slot = (eng, k % NDMASEM)
            o.sem = "dma_%s_%d" % slot
            o.val = 16 * (k // NDMASEM + 1)
            o.marked = True
            prev = self.dma_last.get(slot)
            if prev is not None:
                deps.append((prev, True))
            self.dma_last[slot] = o
        elif kind == "cc":
            self.cc_count += 1
            o.sem = "ccsem"
            o.val = self.cc_count
            o.marked = True
        for d, raw in deps:
            if d is o:
                continue
            if d.kind == "c" and d.eng == eng and eng == "pe":
                continue
            d.marked = True
            o.deps.append(d)
        self.ops[eng].append(o)
        key = self._key(o)
        for b in reads:
            b.readers[key] = o
        for b in writes:
            b.last_w = o
            b.readers = {}
        return o

    def fence(self, olds, news):
        col = {}
        for b in olds:
            for o in ([b.last_w] if b.last_w is not None else []) + list(b.readers.values()):
                k = self._key(o)
                if k not in col or col[k].seq < o.seq:
                    col[k] = o
        for n in news:
            for k, o in col.items():
                if k not in n.readers or n.readers[k].seq < o.seq:
                    n.readers[k] = o

    def emit(self, final_waits=()):
        nc = self.nc
        for e in ENGS:
            c = 0
            for o in self.ops[e]:
                if o.kind == "c" and o.marked:
                    c += 1
                    o.sem = "eng_" + e
                    o.val = c
        semnames = set()
        for e in ENGS:
            for o in self.ops[e]:
                if o.marked:
                    semnames.add(o.sem)
        with ExitStack() as es:
            sems = {n: es.enter_context(nc.semaphore(n)) for n in sorted(semnames)}
            block = es.enter_context(nc.Block())
            handles = {"pe": block.tensor, "act": block.scalar, "dve": block.vector,
                       "pool": block.gpsimd, "sp": block.sync}
            for e in ENGS:
                ops = self.ops[e]
                fw = list(final_waits) if e == "sp" else []
                if not ops and not fw:
                    continue

                def body(eng, ops=ops, fw=fw):
                    seen = {}
                    for o in ops:
                        need = {}
                        for d in o.deps:
                            if need.get(d.sem, 0) < d.val:
                                need[d.sem] = d.val
                        for s, v in need.items():
                            if seen.get(s, 0) >= v:
                                continue
                            seen[s] = v
                            eng.wait_ge(sems[s], v)
                        ins = o.fn(eng)
                        if o.marked:
                            ins.then_inc(sems[o.sem], 16 if o.kind == "dma" else 1)
                    for d in fw:
                        if seen.get(d.sem, 0) < d.val:
                            seen[d.sem] = d.val
                            eng.wait_ge(sems[d.sem], d.val)

                handles[e](body)


class Arena:
    def __init__(self, S, tensor_bf16, nbytes):
        self.S = S
        self.t = tensor_bf16
        self.nbytes = nbytes
        self.live = []
        self.hist = []

    def alloc(self, shape, dtype, nbufs=1, name=""):
        esz = 4 if dtype == F32 else 2
        n = int(np.prod(shape))
        size = (n * esz + 31) // 32 * 32
        off = 0
        for (o, s, _) in sorted(self.live):
            if off + size <= o:
                break
            off = max(off, o + s)
        assert off + size <= self.nbytes, "arena overflow: %s need %d at %d" % (name, size, off)
        bufs = [Buf(name + str(i)) for i in range(nbufs)]
        olds = []
        for (o, s, bs) in self.hist:
            if o < off + size and off < o + s:
                olds.extend(bs)
        if olds:
            self.S.fence(olds, bufs)
        rec = (off, size, bufs)
        self.live.append(rec)
        self.hist.append(rec)
        ap = self.t[:, off // 2: off // 2 + n * esz // 2]
        if dtype == F32:
            ap = ap.bitcast(F32)
        if len(shape) == 2:
            ap = ap.rearrange("p (a b) -> p a b", a=shape[0])
        elif len(shape) == 3:
            ap = ap.rearrange("p (a b c) -> p a b c", a=shape[0], b=shape[1])
        return ap, bufs, rec

    def free(self, rec):
        self.live.remove(rec)


NMC = 1412
QS = 128.0 ** -0.5
LNQS = float(np.log(QS))
RG = [[0, 1, 2, 3], [4, 5, 6, 7]]


def build_program(stage="full", FG=2):
    nc = bass.Bass("TRN2", target_bir_lowering=False)

    def din(name, shape, dt=F32):
        return nc.dram_tensor(name, list(shape), dt, kind="ExternalInput").ap()

    xT = din("xT", [D, NT])
    ctxT = din("ctxT", [D, NCTX])
    cond = din("cond", [128, 16])
    vecs = din("vecs", [128, 104])
    ada_w = din("ada_w", [D, 9 * D])
    f1w1 = din("f1w1", [D, DFF]); f1w3 = din("f1w3", [D, DFF]); f1w2 = din("f1w2", [DFF, D])
    f2w1 = din("f2w1", [D, DFF]); f2w3 = din("f2w3", [D, DFF]); f2w2 = din("f2w2", [DFF, D])
    consts_bf = din("consts_bf", [128, 256])
    w_in = din("w_in", [D, INC])
    w_out = din("w_out", [D, D])
    rowv = din("rowv", [1, 1544])
    wg = din("wg", [32, 512])
    mixc = din("mixc", [128, NMC])
    rope = din("rope", [16, 128, 1024])
    rmask = din("rmask", [1, 8])
    outT = nc.dram_tensor("outT", [D, NT], F32, kind="ExternalOutput").ap()
    ccR_in = nc.dram_tensor("ccR_in", [128, 1024], F32).ap()
    ccR_out = nc.dram_tensor("ccR_out", [512, 1024], F32).ap()
    ccG_in = nc.dram_tensor("ccG_in", [128, 516], F32).ap()
    ccG_out = nc.dram_tensor("ccG_out", [512, 516], F32).ap()

    es = ExitStack()
    with es:
        def sb(name, shape, dt):
            return es.enter_context(nc.sbuf_tensor(name, shape, dt))

        X = sb("X", [128, KD, NT], F32)
        CB = sb("CB", [128, 256], BF16)
        VEC = sb("VEC", [128, 104], F32)
        CND = sb("CND", [128, 16], F32)
        SCOND = sb("SCOND", [128, KD, 2], BF16)
        MODS = sb("MODS", [128, 72, 2], F32)
        MA = sb("MA", [128, 3, KD, 2], F32)
        MG = sb("MG", [128, 3, KD, 2], F32)
        ARENA_BYTES = 144896
        SCR = sb("SCR", [128, ARENA_BYTES // 2], BF16)
        PS = [es.enter_context(nc.psum_tensor("ps%d" % i, [128, 512], F32)) for i in range(8)]

        S = Sched(nc)
        AR = Arena(S, SCR, ARENA_BYTES)
        ident = CB[:, 0:128]
        ones = CB[:, 128:256]

        bX = [[Buf() for t in range(16)] for k in range(KD)]
        bCB, bVEC, bCND, bSC, bMODS, bMA, bMG = (Buf(n) for n in "CB VEC CND SC MODS MA MG".split())
        bPS = [Buf("ps%d" % i, excl=True) for i in range(8)]

        def xb(k, t0, t1):
            return bX[k][t0 // 128:(t1 + 127) // 128]

        def MM(out, lhsT, rhs, start, stop, r, w):
            return S.op("pe", lambda e: e.matmul(out, lhsT, rhs, start=start, stop=stop), r, w)

        def TR(out, in_, r, w):
            return S.op("pe", lambda e: e.transpose(out, in_, ident), list(r) + [bCB], w)

        def TT(eng, out, a, b, op, r, w):
            return S.op(eng, lambda e: e.tensor_tensor(out, a, b, op), r, w)

        def TS(eng, out, a, s1, s2, op0, op1, r, w):
            if s2 is None:
                return S.op(eng, lambda e: e.tensor_scalar(out, a, s1, None, op0), r, w)
            return S.op(eng, lambda e: e.tensor_scalar(out, a, s1, s2, op0, op1), r, w)

        def STT(eng, out, a, s, b, op0, op1, r, w):
            return S.op(eng, lambda e: e.scalar_tensor_tensor(out, a, s, b, op0, op1), r, w)

        def ACTV(out, in_, func, r, w, bias=None, scale=None):
            kw = {}
            if bias is not None:
                kw["bias"] = bias
            if scale is not None:
                kw["scale"] = scale
            return S.op("act", lambda e: e.activation(out, in_, func, **kw), r, w)

        def CP(eng, out, in_, r, w):
            if eng == "act":
                return S.op("act", lambda e: e.activation(out, in_, AF.Copy), r, w)
            return S.op(eng, lambda e: e.tensor_copy(out, in_), r, w)

        def RED(eng, out, in_, r, w):
            return S.op(eng, lambda e: e.tensor_reduce(out, in_, mybir.AxisListType.X, ALU.add), r, w)

        def DMA(eng, out, in_, r, w):
            return S.op(eng, lambda e: e.dma_start(out=out, in_=in_), r, w, kind="dma")

        def bc(ap, axis, shape):
            return ap.unsqueeze(axis).broadcast_to(list(shape))

        DMA("sp", CND[:], cond, [], [bCND])
        DMA("sp", VEC[:], vecs, [], [bVEC])
        DMA("pool", CB[:], consts_bf, [], [bCB])
        XC, bXCl, xc_rec = AR.alloc([KD, NCTX], F32, nbufs=KD, name="XC")
        bXC = [[bXCl[k]] for k in range(KD)]

        ACTV(SCOND[:].rearrange("p k j -> p j k"), CND[:].rearrange("p (j k) -> p j k", j=2), AF.Silu, [bCND], [bSC])
        aw = [AR.alloc([KD, 1024], BF16, name="adaw%d" % i) for i in range(2)]
        mods_ps = PS[7][:, 0:144].rearrange("p (a b) -> p a b", b=2)
        MODS4 = MODS[:].rearrange("p (m k) j -> p m k j", k=KD)

        def ada_mod(m):
            a_ap, a_b, _ = aw[m % 2]
            DMA("pool", a_ap, ada_w[:, m * 1024:(m + 1) * 1024].rearrange("(k p) c -> p k c", p=128), [], [a_b[0]])
            for j in range(8):
                for k in range(KD):
                    MM(mods_ps[:, m * 8 + j, :], a_ap[:, k, j * 128:(j + 1) * 128], SCOND[:, k, :], k == 0, k == KD - 1,
                       [a_b[0], bSC], [bPS[7]])

        def ada_finish(m0, m1, norms):
            TT("dve", MODS[:, m0 * 8:m1 * 8, :], mods_ps[:, m0 * 8:m1 * 8, :],
               bc(VEC[:, 32 + m0 * 8:32 + m1 * 8], 2, [128, (m1 - m0) * 8, 2]), ALU.add, [bPS[7], bVEC], [bMODS])
            for n in norms:
                STT("dve", MA[:, n], MODS4[:, 3 * n + 1], 1.0, bc(VEC[:, 8 * n:8 * n + 8], 2, [128, KD, 2]), ALU.add, ALU.mult,
                    [bMODS, bVEC], [bMA])
                TS("dve", MG[:, n], MODS4[:, 3 * n + 2], 1.0 if n == 1 else 0.5, None, ALU.mult, None, [bMODS], [bMG])

        for m in range(3):
            ada_mod(m)
        ada_finish(0, 3, [0])

        for T in range(4):
            DMA("sp", X[:, :, T * 512:(T + 1) * 512], xT[:, T * 512:(T + 1) * 512].rearrange("(k p) t -> p k t", p=128),
                [], [b for k in range(KD) for b in xb(k, T * 512, T * 512 + 512)])
        DMA("sp", XC, ctxT.rearrange("(k p) t -> p k t", p=128), [], [b for k in range(KD) for b in bXC[k]])

        def normalize(n, h, hb, hc, hcb, with_ctx):
            sq, sqb, sq_rec = AR.alloc([KD, 512], BF16, name="sq")
            rs, rsb, rs_rec = AR.alloc([512], F32, name="rstd")
            tm = [AR.alloc([512], F32, name="tmn%d" % i) for i in range(4)]
            tiles = [("l", T, 512) for T in range(4)] + ([("c", 0, NCTX)] if with_ctx else [])
            ss = PS[7]
            cnt = 0
            for (kind, T, W) in tiles:
                src = X[:, :, T * 512:(T + 1) * 512] if kind == "l" else XC
                srcb = (lambda k, T=T: xb(k, T * 512, T * 512 + 512)) if kind == "l" else (lambda k: bXC[k])
                j = 0 if kind == "l" else 1
                ACTV(sq[:, :, 0:W], src, AF.Square, [b for k in range(KD) for b in srcb(k)], [sqb[0]])
                for k in range(KD):
                    MM(ss[:, 0:W], ones, sq[:, k, 0:W], k == 0, k == KD - 1, [sqb[0], bCB], [bPS[7]])
                ACTV(rs[:, 0:W], ss[:, 0:W], AF.Ln, [bPS[7]], [rsb[0]], bias=EPS, scale=1.0 / D)
                ACTV(rs[:, 0:W], rs[:, 0:W], AF.Exp, [rsb[0]], [rsb[0]], scale=-0.5)
                for k in range(KD):
                    t_ap, t_b, _ = tm[cnt % 4]
                    cnt += 1
                    STT("dve", t_ap[:, 0:W], src[:, k, :], MA[:, n, k, j:j + 1], rs[:, 0:W], ALU.mult, ALU.mult,
                        srcb(k) + [bMA, rsb[0]], [t_b[0]])
                    dst = h[:, k, T * 512:(T + 1) * 512] if kind == "l" else hc[:, k, :]
                    dstb = hb[k * 4 + T] if kind == "l" else hcb[k]
                    ACTV(dst, t_ap[:, 0:W], AF.Identity, [t_b[0], bMODS], [dstb], bias=MODS4[:, 3 * n, k, j:j + 1])
            AR.free(sq_rec); AR.free(rs_rec)
            for t in tm:
                AR.free(t[2])

        def ffn(n, w1, w3, w2, with_ctx, hook=None):
            h, hb, h_rec = AR.alloc([KD, NT], BF16, nbufs=KD * 4, name="h")
            hc = hcb = hc_rec = None
            if with_ctx:
                hc, hcb, hc_rec = AR.alloc([KD, NCTX], BF16, nbufs=KD, name="hc")
            normalize(n, h, hb, hc, hcb, with_ctx)
            tiles = [("l", T, 512) for T in range(4)] + ([("c", 0, NCTX)] if with_ctx else [])
            NG = NFF // FG
            NW = 3
            wb = [(AR.alloc([KD, FG * 128], BF16, name="w1g%d" % i), AR.alloc([KD, FG * 128], BF16, name="w3g%d" % i),
                   AR.alloc([FG, D], BF16, name="w2g%d" % i)) for i in range(NW)]
            sbuf = [AR.alloc([512], F32, name="silu%d" % i) for i in range(2)]
            gbuf = [AR.alloc([FG, 512], BF16, nbufs=FG, name="g%d" % i) for i in range(2)]
            ucnt = ycnt = tcnt = 0
            for G in range(NG):
                (w1a, w1b, _), (w3a, w3b, _), (w2a, w2b, _) = wb[G % NW]
                c0 = G * FG * 128
                DMA("pool", w1a, w1[:, c0:c0 + FG * 128].rearrange("(k p) c -> p k c", p=128), [], [w1b[0]])
                DMA("pool", w3a, w3[:, c0:c0 + FG * 128].rearrange("(k p) c -> p k c", p=128), [], [w3b[0]])
                DMA("pool", w2a, w2[c0:c0 + FG * 128, :].rearrange("(f p) d -> p f d", p=128), [], [w2b[0]])
                if hook is not None:
                    hook(G)
                for (kind, T, W) in tiles:
                    j = 0 if kind == "l" else 1
                    ga, gb, _ = gbuf[tcnt % 2]
                    tcnt += 1
                    for f in range(FG):
                        pu1 = (ucnt % 2) * 2
                        pu3 = pu1 + 1
                        ucnt += 1
                        for (pw, wa, wbb) in ((pu1, w1a, w1b), (pu3, w3a, w3b)):
                            for k in range(KD):
                                hs = h[:, k, T * 512:(T + 1) * 512] if kind == "l" else hc[:, k, :]
                                hbk = hb[k * 4 + T] if kind == "l" else hcb[k]
                                MM(PS[pw][:, 0:W], wa[:, k, f * 128:(f + 1) * 128], hs, k == 0, k == KD - 1,
                                   [wbb[0], hbk], [bPS[pw]])
                        sa, sbb, _ = sbuf[ucnt % 2]
                        ACTV(sa[:, 0:W], PS[pu1][:, 0:W], AF.Silu, [bPS[pu1]], [sbb[0]])
                        TT("dve", ga[:, f, 0:W], PS[pu3][:, 0:W], sa[:, 0:W], ALU.mult, [bPS[pu3], sbb[0]], [gb[f]])
                    for m in range(KD):
                        py = 4 + ycnt % 3
                        ycnt += 1
                        for f in range(FG):
                            MM(PS[py][:, 0:W], w2a[:, f, m * 128:(m + 1) * 128], ga[:, f, 0:W], f == 0, f == FG - 1,
                               [w2b[0], gb[f]], [bPS[py]])
                        dst = X[:, m, T * 512:(T + 1) * 512] if kind == "l" else XC[:, m, :]
                        dstb = xb(m, T * 512, T * 512 + 512) if kind == "l" else bXC[m]
                        STT("dve", dst, PS[py][:, 0:W], MG[:, n, m, j:j + 1], dst, ALU.mult, ALU.add,
                            [bPS[py], bMG] + dstb, dstb)
            for w in wb:
                for a in w:
                    AR.free(a[2])
            for a in sbuf + gbuf:
                AR.free(a[2])
            AR.free(h_rec)
            if with_ctx:
                AR.free(hc_rec)

        def mixing():
            h2, h2b, h2_rec = AR.alloc([KD, NT], BF16, nbufs=KD * 4, name="h2")
            h2c, h2cb, h2c_rec = AR.alloc([KD, NCTX], BF16, nbufs=KD, name="h2c")
            normalize(1, h2, h2b, h2c, h2cb, True)
            AR.free(xc_rec)
            TILES18 = [("c", 0), ("c", 1)] + [("l", t) for t in range(16)]

            def hsrc(kind, t, k):
                if kind == "c":
                    return h2c[:, k, t * 128:(t + 1) * 128], h2cb[k]
                return h2[:, k, t * 128:(t + 1) * 128], h2b[k * 4 + t // 4]

            MIXC, bmx, mixc_rec = AR.alloc([NMC], F32, name="MIXC"); bMIXC = bmx[0]
            DMA("sp", MIXC, mixc, [], [bMIXC])
            ROWV, brv, rowv_rec = AR.alloc([1544], F32, name="ROWV"); bROWV = brv[0]
            DMA("sp", ROWV, rowv.broadcast_to([128, 1544]), [], [bROWV])
            RMK, brm, rmk_rec = AR.alloc([8], F32, name="RMK"); bRMK = brm[0]
            DMA("sp", RMK, rmask.broadcast_to([128, 8]), [], [bRMK])
            SM, bsm, sm_rec = AR.alloc([64], F32, name="SM"); bSM = bsm[0]
            LG = SM[:, 0:8]; KCOL = SM[:, 8:16]; DEC128 = SM[:, 16:24]; D2048M1 = SM[:, 24:32]; TMP8 = SM[:, 32:40]
            c_127mp = MIXC[:, 0:1]; c_p = MIXC[:, 1:2]; negcol = MIXC[:, 2:3]
            ROWIP1 = MIXC[:, 4:132]; ROW128MI = MIXC[:, 132:260]
            DPOS = MIXC[:, 260:388]; DNEG = MIXC[:, 388:516]; UST = MIXC[:, 516:644]; LST = MIXC[:, 644:772]; I2 = MIXC[:, 772:900]
            MU = MIXC[:, 900:1028]; ML = MIXC[:, 1028:1156]; TRIU = MIXC[:, 1156:1284]; TRIL = MIXC[:, 1284:1412]
            ACTV(TMP8, ROWV[:, 1536:1544], AF.Exp, [bROWV], [bSM], scale=-1.0)
            ACTV(TMP8, TMP8, AF.Ln, [bSM], [bSM], bias=1.0)
            TS("dve", LG, TMP8, -1.0, None, ALU.mult, None, [bSM], [bSM])
            for hd in range(8):
                ACTV(KCOL[:, hd:hd + 1], LG[:, hd:hd + 1], AF.Exp, [bSM, bMIXC], [bSM], scale=(c_127mp if hd < 4 else c_p))
            ACTV(DEC128, LG, AF.Exp, [bSM], [bSM], scale=128.0)
            ACTV(D2048M1, LG, AF.Exp, [bSM], [bSM], scale=2048.0)
            TS("dve", D2048M1, D2048M1, -1.0, None, ALU.add, None, [bSM], [bSM])

            R5, br5l, r5_rec = AR.alloc([1024], BF16, name="R5"); bR5 = br5l[0]
            dg, bdgl, dg_rec = AR.alloc([128], BF16, name="dg")
            for m in range(KD):
                TS("dve", dg, ident, MG[:, 1, m, 0:1], None, ALU.mult, None, [bCB, bMG], [bdgl[0]])
                MM(PS[6 + m // 4][:, (m % 4) * 128:(m % 4 + 1) * 128], ones, dg, True, True, [bCB, bdgl[0]], [bPS[6 + m // 4]])
            CP("act", R5[:, 0:512], PS[6][:], [bPS[6]], [bR5])
            CP("act", R5[:, 512:1024], PS[7][:], [bPS[7]], [bR5])
            AR.free(dg_rec)

            RC, brc, rc_rec = AR.alloc([3, 4, 128], F32, name="RCONST"); bRC = brc[0]
            CFTAB, CBTAB, DMASK = RC[:, 0], RC[:, 1], RC[:, 2]
            tmpa, tmpab, tmpa_rec = AR.alloc([128], F32, name="tmpa")
            for hh in range(4):
                ACTV(CFTAB[:, hh, :], ROWIP1, AF.Exp, [bSM, bMIXC], [bRC], scale=LG[:, hh:hh + 1], bias=LNQS)
                ACTV(CBTAB[:, hh, :], ROW128MI, AF.Exp, [bSM, bMIXC], [bRC], scale=LG[:, 4 + hh:5 + hh], bias=LNQS)
                ACTV(DMASK[:, hh, :], DPOS, AF.Exp, [bSM, bMIXC], [bRC], scale=LG[:, hh:hh + 1], bias=LNQS)
                TT("dve", DMASK[:, hh, :], DMASK[:, hh, :], UST, ALU.mult, [bRC, bMIXC], [bRC])
                ACTV(tmpa, DNEG, AF.Exp, [bSM, bMIXC], [tmpab[0]], scale=LG[:, 4 + hh:5 + hh], bias=LNQS)
                TT("dve", tmpa, tmpa, LST, ALU.mult, [tmpab[0], bMIXC], [tmpab[0]])
                TT("dve", DMASK[:, hh, :], DMASK[:, hh, :], tmpa, ALU.add, [bRC, tmpab[0]], [bRC])
                TT("dve", DMASK[:, hh, :], DMASK[:, hh, :], I2, ALU.add, [bRC, bMIXC], [bRC])
            AR.free(tmpa_rec)
            KF = bc(KCOL[:, 0:4], 2, [128, 4, 128]); KBt = bc(KCOL[:, 4:8], 2, [128, 4, 128])
            DECF = DEC128[:, 0:4]; DECB = DEC128[:, 4:8]

            AF_, bAFl, af_rec = AR.alloc([18, 4, 128], BF16, nbufs=18, name="AF")
            AB_, bABl, ab_rec = AR.alloc([18, 4, 128], BF16, nbufs=18, name="AB")

            def rope_ops(src_ps, bsrc, H, Cx, Sgx, bropet, t1, bt1, t2, bt2, dst, bdst, dst_eng):
                TT("dve", t1, src_ps, Cx, ALU.mult, [bsrc, bropet], [bt1])
                s4 = src_ps.rearrange("p (g b c) -> p g b c", b=2, c=32)
                t4 = t2.rearrange("p (g b c) -> p g b c", b=2, c=32)
                g4 = Sgx.rearrange("p (g b c) -> p g b c", b=2, c=32)
                for half in range(2):
                    TT("dve", t4[:, :, half, :], s4[:, :, 1 - half, :], g4[:, :, half, :], ALU.mult, [bsrc, bropet], [bt2])
                TT(dst_eng, dst, t1, t2, ALU.add, [bt1, bt2], [bdst])

            wt, wtb, wt_rec = AR.alloc([KD, 1024], BF16, name="wP1R")
            DMA("pool", wt, w_in[:, 512:1536].rearrange("(k p) c -> p k c", p=128), [], [wtb[0]])
            tb = []
            for i in range(2):
                d = {}
                for nm, shp, dt in (("rp", [1024], F32), ("t1", [512], F32), ("t2", [512], F32), ("kr", [512], F32),
                                    ("kf", [4, 128], BF16), ("kb", [4, 128], BF16), ("vb", [4, 128], BF16)):
                    d[nm] = AR.alloc(shp, dt, name=nm + str(i))
                tb.append(d)
            def p1rA(ci):
                kind, t = TILES18[ci]
                d = tb[ci % 2]
                pa, pb = PS[(ci % 2) * 2], PS[(ci % 2) * 2 + 1]
                bpa, bpb = bPS[(ci % 2) * 2], bPS[(ci % 2) * 2 + 1]
                for k in range(KD):
                    hs, hbk = hsrc(kind, t, k)
                    MM(pa[:], hs, wt[:, k, 0:512], k == 0, k == KD - 1, [hbk, wtb[0]], [bpa])
                for k in range(KD):
                    hs, hbk = hsrc(kind, t, k)
                    MM(pb[:], hs, wt[:, k, 512:1024], k == 0, k == KD - 1, [hbk, wtb[0]], [bpb])
                pa4 = pa[:].rearrange("p (h d) -> p h d", h=4)
                kf, kfb, _ = d["kf"]; kb_, kbb, _ = d["kb"]; vb, vbb, _ = d["vb"]
                if kind == "l":
                    rp, rpb, _ = d["rp"]
                    DMA("sp", rp, rope[t], [], [rpb[0]])
                    t1, t1b, _ = d["t1"]; t2, t2b, _ = d["t2"]; kr, krb, _ = d["kr"]
                    rope_ops(pa[:], bpa, 4, rp[:, 0:512], rp[:, 512:1024], rpb[0], t1, t1b[0], t2, t2b[0], kr, krb[0], "dve")
                    kr3 = kr.rearrange("p (h d) -> p h d", h=4)
                    TT("pool", kf, kr3, KF, ALU.mult, [krb[0], bSM], [kfb[0]])
                    TT("dve", kb_, kr3, KBt, ALU.mult, [krb[0], bSM], [kbb[0]])
                else:
                    TT("dve", kf, pa4, KF, ALU.mult, [bpa, bSM], [kfb[0]])
                    TT("dve", kb_, pa4, KBt, ALU.mult, [bpa, bSM], [kbb[0]])
                CP("act", vb, pb[:].rearrange("p (h d) -> p h d", h=4), [bpb], [vbb[0]])

            def p1rB(ci):
                d = tb[ci % 2]
                kf, kfb, _ = d["kf"]; kb_, kbb, _ = d["kb"]; vb, vbb, _ = d["vb"]
                for hh in range(4):
                    MM(PS[4][:, hh * 128:(hh + 1) * 128], kf[:, hh, :], vb[:, hh, :], True, True, [kfb[0], vbb[0]], [bPS[4]])
                for hh in range(4):
                    MM(PS[5][:, hh * 128:(hh + 1) * 128], kb_[:, hh, :], vb[:, hh, :], True, True, [kbb[0], vbb[0]], [bPS[5]])
                CP("act", AF_[:, ci], PS[4][:].rearrange("p (h d) -> p h d", h=4), [bPS[4]], [bAFl[ci]])
                CP("act", AB_[:, ci], PS[5][:].rearrange("p (h d) -> p h d", h=4), [bPS[5]], [bABl[ci]])

            p1rA(0)
            for ci in range(18):
                if ci + 1 < 18:
                    p1rA(ci + 1)
                p1rB(ci)
            for d in tb:
                for v in d.values():
                    AR.free(v[2])
            AR.free(wt_rec)

            def recur(eng, A, bA, order, dec, bdec, init, store, nm):
                pp = [AR.alloc([4, 128], F32, name=nm + "R%d" % i) for i in range(2)]
                cur = init
                n = 0
                for c in order:
                    nx = pp[n % 2]
                    n += 1
                    if cur is None:
                        CP(eng, nx[0], A[:, c], [bA[c]], [nx[1][0]])
                    else:
                        for hh in range(4):
                            STT(eng, nx[0][:, hh, :], cur[0][:, hh, :], dec[:, hh:hh + 1], A[:, c, hh, :], ALU.mult, ALU.add,
                                [cur[1], bdec, bA[c]], [nx[1][0]])
                        if store:
                            CP("act", A[:, c], cur[0], [cur[1]], [bA[c]])
                    cur = (nx[0], nx[1][0])
                return cur, [p[2] for p in pp]

            LATF = list(range(2, 18)); LATB = list(range(17, 1, -1))
            (Ff, bFf), recs1 = recur("dve", AF_, bAFl, LATF, DECF, bSM, None, False, "ff")
            (Fb, bFb), recs2 = recur("dve", AB_, bABl, LATB, DECB, bSM, None, False, "fb")
            (Cf, bCf), recs3 = recur("dve", AF_, bAFl, [0, 1], DECF, bSM, None, False, "cf")
            (Cb, bCb), recs4 = recur("dve", AB_, bABl, [1, 0], DECB, bSM, None, False, "cb")
            bcc_in, bcc_out = Buf("ccRin"), Buf("ccRout")
            DMA("sp", ccR_in[:, 0:512], Ff.rearrange("p h d -> p (h d)"), [bFf], [bcc_in])
            DMA("sp", ccR_in[:, 512:1024], Fb.rearrange("p h d -> p (h d)"), [bFb], [bcc_in])
            S.op("pool", lambda e: e.collective_compute("AllGather", ALU.bypass, replica_groups=RG, ins=[ccR_in], outs=[ccR_out]),
                 [bcc_in], [bcc_out], kind="cc")
            GR, bgr, gr_rec = AR.alloc([4, 1024], F32, name="GR")
            DMA("sp", GR, ccR_out.rearrange("(r p) c -> p r c", p=128), [bcc_out], [bgr[0]])
            av, bav, av_rec = AR.alloc([4], F32, name="avec")
            for (Sx, bSx, off, ms, rs_) in ((Cf, bCf, 0, 0, (0, 1, 2)), (Cb, bCb, 512, 4, (3, 2, 1))):
                eng = "dve"
                for rr in rs_:
                    mcol = RMK[:, ms + rr:ms + rr + 1]
                    TS(eng, av, D2048M1[:, (0 if off == 0 else 4):(4 if off == 0 else 8)], mcol, 1.0, ALU.mult, ALU.add,
                       [bSM, bRMK], [bav[0]])
                    TT(eng, Sx, Sx, bc(av, 2, [128, 4, 128]), ALU.mult, [bSx, bav[0]], [bSx])
                    STT(eng, Sx, GR[:, rr, off:off + 512].rearrange("p (h d) -> p h d", h=4), mcol, Sx, ALU.mult, ALU.add,
                        [bgr[0], bRMK, bSx], [bSx])
            _, recs5 = recur("dve", AF_, bAFl, LATF, DECF, bSM, (Cf, bCf), True, "pf")
            _, recs6 = recur("dve", AB_, bABl, LATB, DECB, bSM, (Cb, bCb), True, "pb")
            for rl in (recs1, recs2, recs3, recs4, recs5, recs6):
                for r_ in rl:
                    AR.free(r_)
            AR.free(gr_rec); AR.free(av_rec)

            def tail1(o_ps, bo, H, center, normw, gate_sb, bgate, d):
                osb, bosb, _ = d["osb"]; sq, bsq, _ = d["sq"]; st, bst, _ = d["st"]; mg, bmg, _ = d["mg"]
                CP("act", osb, o_ps, [bo], [bosb[0]])
                TT("dve", sq, osb, osb, ALU.mult, [bosb[0]], [bsq[0]])
                s1 = st[:, 0:H]; s2 = st[:, 4:4 + H]; mean = st[:, 8:8 + H]; rstd = st[:, 12:12 + H]; m2 = st[:, 16:16 + H]
                RED("dve", s2, sq, [bsq[0]], [bst[0]])
                if center:
                    RED("dve", s1, osb, [bosb[0]], [bst[0]])
                    TS("dve", mean, s1, 1.0 / 128, None, ALU.mult, None, [bst[0]], [bst[0]])
                    TT("dve", m2, mean, mean, ALU.mult, [bst[0]], [bst[0]])
                    STT("dve", s2, s2, 1.0 / 128, m2, ALU.mult, ALU.subtract, [bst[0]], [bst[0]])
                    ACTV(rstd, s2, AF.Ln, [bst[0]], [bst[0]], bias=EPS)
                else:
                    ACTV(rstd, s2, AF.Ln, [bst[0]], [bst[0]], bias=EPS, scale=1.0 / 128)
                ACTV(rstd, rstd, AF.Exp, [bst[0]], [bst[0]], scale=-0.5)
                nmr = st[:, 20:20 + H]
                if center:
                    STT("dve", nmr, mean, -1.0, rstd, ALU.mult, ALU.mult, [bst[0]], [bst[0]])
                for hh in range(H):
                    if center:
                        ACTV(osb[:, hh, :], osb[:, hh, :], AF.Identity, [bosb[0], bst[0]], [bosb[0]],
                             scale=rstd[:, hh:hh + 1], bias=nmr[:, hh:hh + 1])
                    else:
                        ACTV(osb[:, hh, :], osb[:, hh, :], AF.Identity, [bosb[0], bst[0]], [bosb[0]], scale=rstd[:, hh:hh + 1])
                TT("dve", mg, osb, gate_sb, ALU.mult, [bosb[0], bgate], [bmg[0]])

            def gate_ops(g_ps, bg, normw, d):
                ge, geb, _ = d["ge"]; gt, gtb, _ = d["gt"]
                ACTV(ge, g_ps, AF.Exp, [bg], [geb[0]], scale=-1.0)
                ACTV(ge, ge, AF.Ln, [geb[0]], [geb[0]], bias=1.0)
                ACTV(ge, ge, AF.Exp, [geb[0]], [geb[0]], scale=-1.0)
                TT("dve", ge, g_ps, ge, ALU.mult, [bg, geb[0]], [geb[0]])
                TT("pool", gt, ge, normw, ALU.mult, [geb[0], bROWV], [gtb[0]])

            def tail2(H, wo, bwo, t, d, ptr, bptr, py0):
                mg, bmg, _ = d["mg"]; mT, bmT, _ = d["mT"]
                for hh in range(H):
                    TR(ptr[:, hh, :], mg[:, hh, :], [bmg[0]], [bptr])
                CP("act", mT, ptr[:, 0:H, :], [bptr], [bmT[0]])
                y = [PS[py0][:].rearrange("p (m t) -> p m t", m=4), PS[py0 + 1][:].rearrange("p (m t) -> p m t", m=4)]
                for m in range(KD):
                    for cc in range(H):
                        MM(y[m // 4][:, m % 4, :], wo[:, cc, m * 128:(m + 1) * 128], mT[:, cc, :], cc == 0, cc == H - 1,
                           [bwo, bmT[0]], [bPS[py0 + m // 4]])
                for hf in range(2):
                    dst = X[:, 4 * hf:4 * hf + 4, t * 128:(t + 1) * 128]
                    xbs = [bX[m][t] for m in range(4 * hf, 4 * hf + 4)]
                    TT("dve", dst, y[hf], dst, ALU.add, [bPS[py0 + hf]] + xbs, xbs)

            TAILBUFS = (("ge", [2, 128], F32), ("osb", [2, 128], F32), ("sq", [2, 128], F32), ("st", [24], F32), ("mg", [2, 128], BF16), ("mT", [2, 128], BF16))

            PS4b = PS[4][:].bitcast(BF16)
            for hp in range(2):
                wt, wtb, wt_rec = AR.alloc([KD, 1024], BF16, name="wP2R")
                for gi in range(4):
                    c0 = gi * 512 + hp * 256
                    DMA("pool", wt[:, :, gi * 256:(gi + 1) * 256], w_in[:, c0:c0 + 256].rearrange("(k p) c -> p k c", p=128), [], [wtb[0]])
                wo, wob, wo_rec = AR.alloc([2, D], BF16, name="woR")
                DMA("pool", wo, w_out[hp * 256:(hp + 1) * 256, :].rearrange("(c p) d -> p c d", p=128), [], [wob[0]])
                TT("pool", wo, wo, bc(R5, 1, [128, 2, D]), ALU.mult, [wob[0], bR5], [wob[0]])
                t1s = AR.alloc([512], F32, name="t1s"); t2s = AR.alloc([512], F32, name="t2s")
                tb = []
                for i in range(2):
                    d = {"t1": t1s, "t2": t2s} if i == 0 else {}
                    for nm, shp, dt in (("rp", [1024], F32), ("qr", [512], BF16), ("vb", [2, 128], BF16), ("gt", [2, 128], BF16),
                                        ("qT", [2, 128], BF16), ("qfT", [2, 128], BF16), ("qbT", [2, 128], BF16), ("kT", [2, 128], BF16),
                                        ("pT", [2, 128], BF16)) + TAILBUFS:
                        d[nm] = AR.alloc(shp, dt, name=nm + str(i))
                    tb.append(d)
                def stA(t, hp=hp, wt=wt, wtb=wtb, tb=tb):
                    d = tb[t % 2]
                    pa, pb = PS[(t % 2) * 2], PS[(t % 2) * 2 + 1]
                    bpa, bpb = bPS[(t % 2) * 2], bPS[(t % 2) * 2 + 1]
                    for k in range(KD):
                        hs, hbk = hsrc("l", t, k)
                        MM(pa[:], hs, wt[:, k, 0:512], k == 0, k == KD - 1, [hbk, wtb[0]], [bpa])
                    for k in range(KD):
                        hs, hbk = hsrc("l", t, k)
                        MM(pb[:], hs, wt[:, k, 512:1024], k == 0, k == KD - 1, [hbk, wtb[0]], [bpb])
                    rp, rpb, _ = d["rp"]
                    DMA("sp", rp, rope[t], [], [rpb[0]])
                    t1, t1b, _ = t1s; t2, t2b, _ = t2s
                    qr, qrb, _ = d["qr"]; vb, vbb, _ = d["vb"]
                    rope_ops(pa[:], bpa, 4, rp[:, 0:512], rp[:, 512:1024], rpb[0], t1, t1b[0], t2, t2b[0], qr, qrb[0], "pool")

                def stA2(t, hp=hp, tb=tb):
                    d = tb[t % 2]
                    pb, bpb = PS[(t % 2) * 2 + 1], bPS[(t % 2) * 2 + 1]
                    vb, vbb, _ = d["vb"]
                    CP("act", vb, pb[:, 0:256].rearrange("p (h d) -> p h d", h=2), [bpb], [vbb[0]])
                    gate_ops(pb[:, 256:512].rearrange("p (h d) -> p h d", h=2), bpb,
                             ROWV[:, hp * 256:(hp + 1) * 256].rearrange("p (h e) -> p h e", h=2), d)

                def stB(t, hp=hp, wo=wo, wob=wob, tb=tb):
                    ci = t + 2
                    d = tb[t % 2]
                    qr, qrb, _ = d["qr"]
                    half = (t % 2) * 512
                    ptq = PS4b[:, half:half + 512].rearrange("p (a t) -> p a t", a=4)
                    for a_ in range(4):
                        TR(ptq[:, a_, :], qr[:, a_ * 128:(a_ + 1) * 128], [qrb[0]], [bPS[4]])
                    qT, qTb, _ = d["qT"]; qfT, qfTb, _ = d["qfT"]; qbT, qbTb, _ = d["qbT"]; kT, kTb, _ = d["kT"]
                    CP("act", qT, ptq[:, 0:2, :], [bPS[4]], [qTb[0]])
                    TT("dve", qfT, ptq[:, 0:2, :], CFTAB[:, 2 * hp:2 * hp + 2, :], ALU.mult, [bPS[4], bRC], [qfTb[0]])
                    TT("dve", qbT, ptq[:, 0:2, :], CBTAB[:, 2 * hp:2 * hp + 2, :], ALU.mult, [bPS[4], bRC], [qbTb[0]])
                    CP("act", kT, ptq[:, 2:4, :], [bPS[4]], [kTb[0]])
                    vb, vbb, _ = d["vb"]; gt, gtb, _ = d["gt"]; pT, pTb, _ = d["pT"]
                    sc = PS[5][:, 0:256].rearrange("p (h i) -> p h i", h=2)
                    o_ps = PS[5][:, 256:512].rearrange("p (h e) -> p h e", h=2)
                    for hh in range(2):
                        MM(sc[:, hh, :], kT[:, hh, :], qT[:, hh, :], True, True, [kTb[0], qTb[0]], [bPS[5]])
                    TT("dve", pT, sc, DMASK[:, 2 * hp:2 * hp + 2, :], ALU.mult, [bPS[5], bRC], [pTb[0]])
                    for hh in range(2):
                        hd = 2 * hp + hh
                        MM(o_ps[:, hh, :], pT[:, hh, :], vb[:, hh, :], True, False, [pTb[0], vbb[0]], [bPS[5]])
                        MM(o_ps[:, hh, :], qfT[:, hh, :], AF_[:, ci, hd, :], False, False, [qfTb[0], bAFl[ci]], [bPS[5]])
                        MM(o_ps[:, hh, :], qbT[:, hh, :], AB_[:, ci, hd, :], False, True, [qbTb[0], bABl[ci]], [bPS[5]])
                    tail1(o_ps, bPS[5], 2, True, ROWV[:, hp * 256:(hp + 1) * 256].rearrange("p (h e) -> p h e", h=2),
                          gt, gtb[0], d)

                def stC(t, wo=wo, wob=wob, tb=tb):
                    ptm = PS[7][:].bitcast(BF16)[:, 0:256].rearrange("p (a t) -> p a t", a=2)
                    tail2(2, wo, wob[0], t, tb[t % 2], ptm, bPS[7], 6)

                for s_ in range(18):
                    if s_ < 16:
                        stA(s_)
                    if s_ - 2 >= 0:
                        stC(s_ - 2)
                    if 0 <= s_ - 1 < 16:
                        stB(s_ - 1)
                    if s_ < 16:
                        stA2(s_)
                for d in tb:
                    for v in d.values():
                        AR.free(v[2])
                AR.free(wt_rec); AR.free(wo_rec)
            AR.free(af_rec); AR.free(ab_rec); AR.free(rc_rec)

            WG, bwgl, wg_rec = AR.alloc([512], BF16, name="WG"); bWG = bwgl[0]
            DMA("pool", WG[0:32, :], wg, [], [bWG])
            ET, betl, et_rec = AR.alloc([18, 4], F32, name="ETOT"); bET = betl[0]
            AG, bAGl, ag_rec = AR.alloc([18, 4, 128], BF16, nbufs=18, name="AG")
            GBIAS = ROWV[:, 1024:1536]

            def gates(z_ps, bz, b_ps, bb_, lrT_ps, blr, W, wgs, bias, d, need_tot, tot_ps, btot, etdst):
                lr, lrb, _ = d["lr"]; zb, zbb, _ = d["zb"]
                if lrT_ps is not None:
                    CP("act", lr[0:32, :], lrT_ps, [blr], [lrb[0]])
                MM(z_ps, lr[0:32, :], wgs, True, True, [lrb[0], bWG], [bz])
                TT("dve", zb, z_ps, bias, ALU.add, [bz, bROWV], [zbb[0]])
                ACTV(zb, zb, AF.Exp, [zbb[0]], [zbb[0]], scale=-1.0)
                ACTV(zb, zb, AF.Ln, [zbb[0]], [zbb[0]], bias=1.0)
                nh = W // 128
                sp4 = zb.rearrange("p (h a d) -> p h a d", h=nh, a=2)
                b4 = b_ps.rearrange("p (h a d) -> p h a d", h=nh, a=2)
                MM(b4[:, :, 0, :], MU, sp4[:, :, 0, :], True, True, [zbb[0], bMIXC], [bb_])
                MM(b4[:, :, 1, :], ML, sp4[:, :, 1, :], True, True, [zbb[0], bMIXC], [bb_])
                if need_tot:
                    for hh in range(nh):
                        MM(tot_ps[:, hh:hh + 1], zb[:, hh * 128:(hh + 1) * 128], negcol, True, True, [zbb[0], bMIXC], [btot])
                    ACTV(etdst, tot_ps[:, 0:nh], AF.Exp, [btot], [bET])

            wt, wtb, wt_rec = AR.alloc([KD, 800], BF16, name="wP1G")
            DMA("pool", wt[:, :, 0:768], w_in[:, 2304:3072].rearrange("(k p) c -> p k c", p=128), [], [wtb[0]])
            DMA("pool", wt[:, :, 768:800], w_in[:, 3584:3616].rearrange("(k p) c -> p k c", p=128), [], [wtb[0]])
            tb = []
            for i in range(2):
                d = {}
                for nm, shp, dt in (("lr", [128], BF16), ("zb", [512], F32), ("en", [512], F32), ("kt", [512], BF16), ("vb", [4, 128], BF16)):
                    d[nm] = AR.alloc(shp, dt, name=nm + str(i))
                tb.append(d)
            def p1gA1(ci):
                kind, t = TILES18[ci]
                d = tb[ci % 2]
                pa, pb = PS[(ci % 2) * 2], PS[(ci % 2) * 2 + 1]
                bpa, bpb = bPS[(ci % 2) * 2], bPS[(ci % 2) * 2 + 1]
                for k in range(KD):
                    hs, hbk = hsrc(kind, t, k)
                    MM(pa[:, 0:256], hs, wt[:, k, 0:256], k == 0, k == KD - 1, [hbk, wtb[0]], [bpa])
                for k in range(KD):
                    hs, hbk = hsrc(kind, t, k)
                    MM(pb[:], hs, wt[:, k, 256:768], k == 0, k == KD - 1, [hbk, wtb[0]], [bpb])
                for k in range(KD):
                    hs, hbk = hsrc(kind, t, k)
                    MM(pa[0:32, 256:384], wt[:, k, 768:800], hs, k == 0, k == KD - 1, [hbk, wtb[0]], [bpa])
                lr, lrb, _ = d["lr"]
                CP("act", lr[0:32, :], pa[0:32, 256:384], [bpa], [lrb[0]])

            def p1gA2(ci):
                d = tb[ci % 2]
                pa, pb = PS[(ci % 2) * 2], PS[(ci % 2) * 2 + 1]
                bpa, bpb = bPS[(ci % 2) * 2], bPS[(ci % 2) * 2 + 1]
                gates(PS[4][:], bPS[4], PS[5][:], bPS[5], None, None, 512, WG[0:32, :], GBIAS, d, True,
                      pa[:, 384:388], bpa, ET[:, ci, :])
                en, enb, _ = d["en"]; kt, ktb, _ = d["kt"]; vb, vbb, _ = d["vb"]
                ACTV(en, PS[5][:], AF.Exp, [bPS[5]], [enb[0]], scale=-1.0)
                k3 = pa[:, 0:256].rearrange("p (h d) -> p h d", h=4)
                kt4 = kt.rearrange("p (h a d) -> p h a d", h=4, a=2); en4 = en.rearrange("p (h a d) -> p h a d", h=4, a=2)
                for a_ in range(2):
                    TT("dve", kt4[:, :, a_, :], k3, en4[:, :, a_, :], ALU.mult, [bpa, enb[0]], [ktb[0]])
                CP("act", vb, pb[:].rearrange("p (h e) -> p h e", h=4), [bpb], [vbb[0]])

            def p1gB(ci):
                d = tb[ci % 2]
                kt, ktb, _ = d["kt"]; vb, vbb, _ = d["vb"]
                for hh in range(4):
                    MM(PS[6][:, hh * 128:(hh + 1) * 128], kt[:, hh * 128:(hh + 1) * 128], vb[:, hh, :], True, True,
                       [ktb[0], vbb[0]], [bPS[6]])
                CP("dve", AG[:, ci], PS[6][:].rearrange("p (h e) -> p h e", h=4), [bPS[6]], [bAGl[ci]])

            for it in range(20):
                if it < 18:
                    p1gA1(it)
                if 0 <= it - 1 < 18:
                    p1gA2(it - 1)
                if 0 <= it - 2 < 18:
                    p1gB(it - 2)
            for d in tb:
                for v in d.values():
                    AR.free(v[2])
            AR.free(wt_rec)

            def grecur(eng, lo, order, init, store, nm):
                pp = [AR.alloc([4, 128], F32, name=nm + "R%d" % i) for i in range(2)]
                sl = slice(lo, lo + 64)
                cur = init
                n = 0
                for c in order:
                    nx = pp[n % 2]
                    n += 1
                    Ec = bc(ET[sl, c, :], 2, [64, 4, 128])
                    if cur is None:
                        TT(eng, nx[0][sl], AG[sl, c], Ec, ALU.mult, [bAGl[c], bET], [nx[1][0]])
                    else:
                        TT(eng, nx[0][sl], cur[0][sl], AG[sl, c], ALU.add, [cur[1], bAGl[c]], [nx[1][0]])
                        TT(eng, nx[0][sl], nx[0][sl], Ec, ALU.mult, [nx[1][0], bET], [nx[1][0]])
                        if store:
                            CP("act", AG[sl, c], cur[0][sl], [cur[1]], [bAGl[c]])
                    cur = (nx[0], nx[1][0])
                return cur, [p[2] for p in pp]

            (Gf, bGf), g1 = grecur("dve", 0, LATF, None, False, "gf")
            (Gb, bGb), g2 = grecur("pool", 64, LATB, None, False, "gb")
            (GCf, bGCf), g3 = grecur("dve", 0, [0, 1], None, False, "gcf")
            (GCb, bGCb), g4 = grecur("pool", 64, [1, 0], None, False, "gcb")
            DC, bdcl, dc_rec = AR.alloc([4], F32, name="DCORE")
            CP("dve", DC, ET[:, 2, :], [bET], [bdcl[0]])
            for c in range(3, 18):
                TT("dve", DC, DC, ET[:, c, :], ALU.mult, [bdcl[0], bET], [bdcl[0]])
            bg_in, bg_out = Buf("ccGin"), Buf("ccGout")
            DMA("sp", ccG_in[0:64, 0:512], Gf[0:64].rearrange("p h d -> p (h d)"), [bGf], [bg_in])
            DMA("sp", ccG_in[64:128, 0:512], Gb[64:128].rearrange("p h d -> p (h d)"), [bGb], [bg_in])
            DMA("sp", ccG_in[:, 512:516], DC, [bdcl[0]], [bg_in])
            S.op("pool", lambda e: e.collective_compute("AllGather", ALU.bypass, replica_groups=RG, ins=[ccG_in], outs=[ccG_out]),
                 [bg_in], [bg_out], kind="cc")
            GG, bggl, gg_rec = AR.alloc([4, 516], F32, name="GG")
            DMA("sp", GG, ccG_out.rearrange("(r p) c -> p r c", p=128), [bg_out], [bggl[0]])
            av, bav, av_rec = AR.alloc([4], F32, name="gavec")
            for (Sx, bSx, lo, ms, rs_, eng) in ((GCf, bGCf, 0, 0, (0, 1, 2), "dve"), (GCb, bGCb, 64, 4, (3, 2, 1), "dve")):
                sl = slice(lo, lo + 64)
                for rr in rs_:
                    mcol = RMK[sl, ms + rr:ms + rr + 1]
                    TS(eng, av[sl], GG[sl, rr, 512:516], -1.0, mcol, ALU.add, ALU.mult, [bggl[0], bRMK], [bav[0]])
                    TS(eng, av[sl], av[sl], 1.0, None, ALU.add, None, [bav[0]], [bav[0]])
                    TT(eng, Sx[sl], Sx[sl], bc(av[sl], 2, [64, 4, 128]), ALU.mult, [bSx, bav[0]], [bSx])
                    STT(eng, Sx[sl], GG[sl, rr, 0:512].rearrange("p (h d) -> p h d", h=4), mcol, Sx[sl], ALU.mult, ALU.add,
                        [bggl[0], bRMK, bSx], [bSx])
            _, g5 = grecur("dve", 0, LATF, (GCf, bGCf), True, "gpf")
            _, g6 = grecur("pool", 64, LATB, (GCb, bGCb), True, "gpb")
            for rl in (g1, g2, g3, g4, g5, g6):
                for r_ in rl:
                    AR.free(r_)
            AR.free(gg_rec); AR.free(av_rec); AR.free(dc_rec); AR.free(h2c_rec)

            for hp in range(2):
                wt, wtb, wt_rec = AR.alloc([KD, 800], BF16, name="wP2G")
                for (o0, c0, wd) in ((0, 2048 + hp * 128, 128), (128, 2304 + hp * 128, 128), (256, 2560 + hp * 256, 256),
                                     (512, 3072 + hp * 256, 256), (768, 3584, 32)):
                    DMA("pool", wt[:, :, o0:o0 + wd], w_in[:, c0:c0 + wd].rearrange("(k p) c -> p k c", p=128), [], [wtb[0]])
                wo, wob, wo_rec = AR.alloc([2, D], BF16, name="woG")
                DMA("pool", wo, w_out[512 + hp * 256:512 + (hp + 1) * 256, :].rearrange("(c p) d -> p c d", p=128), [], [wob[0]])
                TT("pool", wo, wo, bc(R5, 1, [128, 2, D]), ALU.mult, [wob[0], bR5], [wob[0]])
                tb = []
                for i in range(2):
                    d = {}
                    for nm, shp, dt in (("lr", [128], BF16), ("zb", [256], F32), ("ep", [256], F32), ("en", [256], F32),
                                        ("qt", [2, 128], BF16), ("kt", [2, 128], BF16), ("vb", [2, 128], BF16), ("gt", [2, 128], BF16),
                                        ("qkT", [4, 128], BF16), ("pf", [2, 128], BF16), ("pb", [2, 128], BF16)) + TAILBUFS:
                        d[nm] = AR.alloc(shp, dt, name=nm + str(i))
                    tb.append(d)
                def stA(t, hp=hp, wt=wt, wtb=wtb, tb=tb):
                    d = tb[t % 2]
                    pa, pb = PS[(t % 2) * 2], PS[(t % 2) * 2 + 1]
                    bpa, bpb = bPS[(t % 2) * 2], bPS[(t % 2) * 2 + 1]
                    for k in range(KD):
                        hs, hbk = hsrc("l", t, k)
                        MM(pa[:], hs, wt[:, k, 0:512], k == 0, k == KD - 1, [hbk, wtb[0]], [bpa])
                    for k in range(KD):
                        hs, hbk = hsrc("l", t, k)
                        MM(pb[:, 0:256], hs, wt[:, k, 512:768], k == 0, k == KD - 1, [hbk, wtb[0]], [bpb])
                    for k in range(KD):
                        hs, hbk = hsrc("l", t, k)
                        MM(pb[0:32, 256:384], wt[:, k, 768:800], hs, k == 0, k == KD - 1, [hbk, wtb[0]], [bpb])
                    z_ps = PS[4][:, 0:256]; b_ps = PS[4][:, 256:512]
                    gates(z_ps, bPS[4], b_ps, bPS[4], pb[0:32, 256:384], bpb, 256, WG[0:32, hp * 256:(hp + 1) * 256],
                          GBIAS[:, hp * 256:(hp + 1) * 256], d, False, None, None, None)
                    ep, epb, _ = d["ep"]; en, enb, _ = d["en"]
                    ACTV(ep, b_ps, AF.Exp, [bPS[4]], [epb[0]])
                    ACTV(en, b_ps, AF.Exp, [bPS[4]], [enb[0]], scale=-1.0)
                    qt, qtb, _ = d["qt"]; kt, ktb, _ = d["kt"]; vb, vbb, _ = d["vb"]; gt, gtb, _ = d["gt"]
                    q3 = pa[:, 0:128].rearrange("p (h d) -> p h d", h=2)
                    k3 = pa[:, 128:256].rearrange("p (h d) -> p h d", h=2)
                    qt4 = qt.rearrange("p h (a d) -> p h a d", a=2); kt4 = kt.rearrange("p h (a d) -> p h a d", a=2)
                    ep4 = ep.rearrange("p (h a d) -> p h a d", h=2, a=2); en4 = en.rearrange("p (h a d) -> p h a d", h=2, a=2)
                    for a_ in range(2):
                        STT("dve", qt4[:, :, a_, :], q3, 0.125, ep4[:, :, a_, :], ALU.mult, ALU.mult, [bpa, epb[0]], [qtb[0]])
                        TT("dve", kt4[:, :, a_, :], k3, en4[:, :, a_, :], ALU.mult, [bpa, enb[0]], [ktb[0]])

                def stA2(t, hp=hp, tb=tb):
                    d = tb[t % 2]
                    pa, pb = PS[(t % 2) * 2], PS[(t % 2) * 2 + 1]
                    bpa, bpb = bPS[(t % 2) * 2], bPS[(t % 2) * 2 + 1]
                    vb, vbb, _ = d["vb"]
                    CP("act", vb, pa[:, 256:512].rearrange("p (h e) -> p h e", h=2), [bpa], [vbb[0]])
                    gate_ops(pb[:, 0:256].rearrange("p (h e) -> p h e", h=2), bpb,
                             ROWV[:, 512 + hp * 256:512 + (hp + 1) * 256].rearrange("p (h e) -> p h e", h=2), d)

                def stB(t, hp=hp, wo=wo, wob=wob, tb=tb):
                    ci = t + 2
                    d = tb[t % 2]
                    qt, qtb, _ = d["qt"]; kt, ktb, _ = d["kt"]
                    ptq = PS[5][:].bitcast(BF16)[:, 0:512].rearrange("p (a t) -> p a t", a=4)
                    for hh in range(2):
                        TR(ptq[:, hh, :], qt[:, hh, :], [qtb[0]], [bPS[5]])
                        TR(ptq[:, 2 + hh, :], kt[:, hh, :], [ktb[0]], [bPS[5]])
                    qkT, qkTb, _ = d["qkT"]
                    CP("act", qkT, ptq, [bPS[5]], [qkTb[0]])
                    vb, vbb, _ = d["vb"]; gt, gtb, _ = d["gt"]
                    scf = PS[6][:, 0:256].rearrange("p (h i) -> p h i", h=2)
                    scb = PS[7][:, 0:256].rearrange("p (h i) -> p h i", h=2)
                    for hh in range(2):
                        MM(scf[:, hh, :], qkT[0:64, 2 + hh, :], qkT[0:64, hh, :], True, True, [qkTb[0]], [bPS[6]])
                    for hh in range(2):
                        MM(scb[:, hh, :], qkT[64:128, 2 + hh, :], qkT[64:128, hh, :], True, True, [qkTb[0]], [bPS[7]])
                    pf, pfb, _ = d["pf"]; pbt, pbb, _ = d["pb"]
                    TT("dve", pf, scf, bc(TRIU, 1, [128, 2, 128]), ALU.mult, [bPS[6], bMIXC], [pfb[0]])
                    TT("dve", pbt, scb, bc(TRIL, 1, [128, 2, 128]), ALU.mult, [bPS[7], bMIXC], [pbb[0]])
                    o_ps = PS[6][:, 256:512].rearrange("p (h e) -> p h e", h=2)
                    for hh in range(2):
                        hd = 2 * hp + hh
                        MM(o_ps[:, hh, :], pf[:, hh, :], vb[:, hh, :], True, False, [pfb[0], vbb[0]], [bPS[6]])
                        MM(o_ps[:, hh, :], pbt[:, hh, :], vb[:, hh, :], False, False, [pbb[0], vbb[0]], [bPS[6]])
                        MM(o_ps[:, hh, :], qkT[:, hh, :], AG[:, ci, hd, :], False, True, [qkTb[0], bAGl[ci]], [bPS[6]])
                    tail1(o_ps, bPS[6], 2, False, ROWV[:, 512 + hp * 256:512 + (hp + 1) * 256].rearrange("p (h e) -> p h e", h=2),
                          gt, gtb[0], d)

                def stC(t, wo=wo, wob=wob, tb=tb):
                    ptm = PS[7][:].bitcast(BF16)[:, 512:768].rearrange("p (a t) -> p a t", a=2)
                    tail2(2, wo, wob[0], t, tb[t % 2], ptm, bPS[7], 6)

                for s_ in range(18):
                    if s_ < 16:
                        stA(s_)
                    if s_ - 2 >= 0:
                        stC(s_ - 2)
                    if 0 <= s_ - 1 < 16:
                        stB(s_ - 1)
                    if s_ < 16:
                        stA2(s_)
                for d in tb:
                    for v in d.values():
                        AR.free(v[2])
                AR.free(wt_rec); AR.free(wo_rec)
            for r_ in (wg_rec, et_rec, ag_rec, mixc_rec, rowv_rec, rmk_rec, sm_rec, h2_rec, r5_rec):
                AR.free(r_)

        def final_norm():
            sq, sqb, sq_rec = AR.alloc([KD, 512], BF16, name="fsq")
            rs, rsb, rs_rec = AR.alloc([512], F32, name="frstd")
            for T in range(4):
                src = X[:, :, T * 512:(T + 1) * 512]
                allb = [b for k in range(KD) for b in xb(k, T * 512, T * 512 + 512)]
                ACTV(sq, src, AF.Square, allb, [sqb[0]])
                for k in range(KD):
                    MM(PS[7][:], ones, sq[:, k, :], k == 0, k == KD - 1, [sqb[0], bCB], [bPS[7]])
                ACTV(rs, PS[7][:], AF.Ln, [bPS[7]], [rsb[0]], bias=EPS, scale=1.0 / D)
                ACTV(rs, rs, AF.Exp, [rsb[0]], [rsb[0]], scale=-0.5)
                for k in range(KD):
                    STT("dve", src[:, k, :], src[:, k, :], VEC[:, 24 + k:25 + k], rs, ALU.mult, ALU.mult,
                        xb(k, T * 512, T * 512 + 512) + [bVEC, rsb[0]], xb(k, T * 512, T * 512 + 512))

        ffn(0, f1w1, f1w3, f1w2, True, hook=lambda G: ada_mod(3 + G) if G < 6 else None)
        ada_finish(3, 9, [1, 2])
        for a in aw:
            AR.free(a[2])
        if stage in ("mix", "ffn2", "full"):
            mixing()
        else:
            AR.free(xc_rec)
        if stage in ("ffn2", "full"):
            ffn(2, f2w1, f2w3, f2w2, False)
        if stage == "full":
            final_norm()

        outs = []
        for T in range(4):
            outs.append(DMA("sp", outT[:, T * 512:(T + 1) * 512].rearrange("(k p) t -> p k t", p=128),
                            X[:, :, T * 512:(T + 1) * 512],
                            [b for k in range(KD) for b in xb(k, T * 512, T * 512 + 512)], []))
        S.emit(final_waits=outs)
    return nc


def _fm(v):
    return np.ascontiguousarray(np.asarray(v, np.float32).reshape(KD, 128).T)


def _mix_consts():
    m = np.zeros((128, NMC), np.float32)
    p = np.arange(128, dtype=np.float32)
    m[:, 0] = 127 - p; m[:, 1] = p; m[:, 2] = -1.0 / 16
    m[:, 4:132] = (p + 1)[None, :]; m[:, 132:260] = (128 - p)[None, :]
    j = p[:, None]; i = p[None, :]
    m[:, 260:388] = np.maximum(i - j, 0); m[:, 388:516] = np.maximum(j - i, 0)
    m[:, 516:644] = (i > j); m[:, 644:772] = (j > i); m[:, 772:900] = 2.0 * QS * np.eye(128)
    m[:, 900:1028] = (j <= i) * (-1.0 / 16); m[:, 1028:1156] = (j >= i) * (-1.0 / 16)
    m[:, 1156:1284] = (j <= i); m[:, 1284:1412] = (j >= i)
    return m


def _rope_tables(q):
    pos = q * NT + np.arange(NT)
    row = (pos // 64).astype(np.float64); col = (pos % 64).astype(np.float64)
    freqs = 10000.0 ** (-np.arange(32, dtype=np.float64) / 32.0)
    ar = row[:, None] * freqs[None, :]; ac = col[:, None] * freqs[None, :]
    cr, sr, cc, sc = np.cos(ar), np.sin(ar), np.cos(ac), np.sin(ac)
    C = np.concatenate([cr, cr, cc, cc], 1); Sg = np.concatenate([-sr, sr, -sc, sc], 1)
    return np.ascontiguousarray(np.concatenate([C] * 4 + [Sg] * 4, 1).reshape(16, 128, 1024).astype(np.float32))


def make_in_maps(inp):
    f32 = lambda a: np.ascontiguousarray(np.asarray(a, np.float32))
    vecs = np.zeros((128, 104), np.float32)
    vecs[:, 0:8] = _fm(inp["norm1_w"][0]); vecs[:, 8:16] = _fm(inp["norm2_w"][0])
    vecs[:, 16:24] = _fm(inp["norm3_w"][0]); vecs[:, 24:32] = _fm(inp["final_norm_w"])
    vecs[:, 32:104] = np.asarray(inp["ada_b"][0], np.float32).reshape(72, 128).T
    cb = np.zeros((128, 256), np.float32)
    cb[:, 0:128] = np.eye(128); cb[:, 128:256] = 1.0
    rowv = np.zeros((1, 1544), np.float32)
    rowv[0, 0:512] = inp["ret_norm_w"][0]; rowv[0, 512:1024] = inp["gla_norm_w"][0]
    gb = np.zeros((4, 2, 64), np.float32)
    gb[:, 0, :] = np.asarray(inp["gla_gate_b_f"][0]).reshape(4, 64); gb[:, 1, :] = np.asarray(inp["gla_gate_b_b"][0]).reshape(4, 64)
    rowv[0, 1024:1536] = gb.reshape(-1)
    rowv[0, 1536:1540] = inp["ret_decay_f"][0]; rowv[0, 1540:1544] = inp["ret_decay_b"][0]
    wgm = np.zeros((32, 4, 2, 64), np.float32)
    wgm[0:16, :, 0, :] = np.asarray(inp["gla_gate_w_f"][0]).reshape(16, 4, 64)
    wgm[16:32, :, 1, :] = np.asarray(inp["gla_gate_w_b"][0]).reshape(16, 4, 64)
    shared = {
        "vecs": vecs, "consts_bf": cb, "ada_w": f32(inp["ada_w"][0]),
        "f1w1": f32(inp["ffn1_w1"][0]), "f1w3": f32(inp["ffn1_w3"][0]), "f1w2": f32(inp["ffn1_w2"][0]),
        "f2w1": f32(inp["ffn2_w1"][0]), "f2w3": f32(inp["ffn2_w3"][0]), "f2w2": f32(inp["ffn2_w2"][0]),
        "w_in": f32(inp["w_in"][0]), "w_out": f32(inp["w_out"][0]), "rowv": rowv,
        "wg": np.ascontiguousarray(wgm.reshape(32, 512)), "mixc": _mix_consts(),
    }
    maps = []
    x = np.asarray(inp["x"], np.float32); ctx = np.asarray(inp["ctx"], np.float32)
    for r in range(NCORES):
        b, q = r // 4, r % 4
        m = dict(shared)
        m["xT"] = np.ascontiguousarray(x[b, q * NT:(q + 1) * NT, :].T)
        m["ctxT"] = np.ascontiguousarray(ctx[b].T)
        cnd = np.zeros((128, 16), np.float32)
        cnd[:, 0:8] = _fm(inp["c"][b]); cnd[:, 8:16] = _fm(inp["c_ctx"])
        m["cond"] = cnd
        m["rope"] = _rope_tables(q)
        rm = np.zeros((1, 8), np.float32)
        for rr in range(4):
            rm[0, rr] = 1.0 if rr < q else 0.0
            rm[0, 4 + rr] = 1.0 if rr > q else 0.0
        m["rmask"] = rm
        maps.append(m)
    return maps


def run(inp, stage="full", **kw):
    nc = build_program(stage, **kw)
    maps = make_in_maps(inp)
    res = run_bass_kernel_spmd(nc, maps, core_ids=list(range(NCORES)))
    out = np.empty((2, 4 * NT, D), np.float32)
    for r in range(NCORES):
        b, q = r // 4, r % 4
        out[b, q * NT:(q + 1) * NT, :] = res.results[r]["outT"].T
    return out


def kernel(**inputs):
    return run(inputs, "full")
```
